# Optimizing a Trainium2 kernel written in Bass

```python
import math
import jax, jax.numpy as jnp
from jax import lax
import numpy as np

D_MODEL = 2048
BATCH = 4
SEQ = 4096
DEPTH = 2

MEM_LEN = 256
CHUNK = 128
WINDOW = 128
G_GROUPS = 4
G_WIDTH = 1024
G_GROUP_DIM = G_WIDTH // G_GROUPS
SWA_HEADS = 16
SWA_KV_HEADS = 4
SWA_HEAD_DIM = 64
SWA_REP = SWA_HEADS // SWA_KV_HEADS
SWA_WIDTH = SWA_HEADS * SWA_HEAD_DIM
SWA_KV_WIDTH = SWA_KV_HEADS * SWA_HEAD_DIM
MEM_HEADS = 4
MEM_HEAD_DIM = 256
MEM_WIDTH = MEM_HEADS * MEM_HEAD_DIM
N_BRANCH = 3
BRANCH_WIDTH = 1024
D_FF = 5504
EPS = 1e-6
NEG = -1e30
IN_WIDTH = 2 * G_WIDTH + SWA_WIDTH + 2 * SWA_KV_WIDTH + MEM_WIDTH + N_BRANCH * D_MODEL

kernel_name = "hybrid_gated_gmlp_swa_memattn_macaron"


def rmsnorm(x, g):
    xf = x.astype(jnp.float32)
    y = xf * lax.rsqrt(jnp.mean(xf * xf, axis=-1, keepdims=True) + EPS)
    return (y * g.astype(jnp.float32)).astype(x.dtype)


def layernorm(x, g, b):
    xf = x.astype(jnp.float32)
    mu = jnp.mean(xf, axis=-1, keepdims=True)
    xc = xf - mu
    var = jnp.mean(xc * xc, axis=-1, keepdims=True)
    y = xc * lax.rsqrt(var + EPS) * g.astype(jnp.float32) + b.astype(jnp.float32)
    return y.astype(x.dtype)


def swiglu(x, w_in, w_out):
    a, b = jnp.split(x @ w_in, 2, axis=-1)
    return (jax.nn.silu(a) * b) @ w_out


def gmlp_branch(u, v, w_s, b_s, ln_g, ln_b):
    B, S, _ = u.shape
    n_chunks = S // CHUNK
    vn = layernorm(v, ln_g, ln_b).reshape(B, n_chunks, CHUNK, G_GROUPS, G_GROUP_DIM)
    causal = jnp.tril(jnp.ones((CHUNK, CHUNK), dtype=bool))
    ws = jnp.where(causal[None], w_s, jnp.zeros((), w_s.dtype))
    mixed = jnp.einsum('gts,bcsgd->bctgd', ws, vn) + jnp.transpose(b_s)[None, None, :, :, None]
    return u * mixed.reshape(B, S, G_WIDTH)


def swa_branch(q, k, v, sinks):
    B, S = q.shape[0], q.shape[1]
    nb = S // WINDOW
    qb = q.reshape(B, nb, WINDOW, SWA_KV_HEADS, SWA_REP, SWA_HEAD_DIM)

    def band(t):
        prev = jnp.pad(t, ((0, 0), (WINDOW, 0), (0, 0), (0, 0)))[:, :S]
        prev = prev.reshape(B, nb, WINDOW, SWA_KV_HEADS, SWA_HEAD_DIM)
        cur = t.reshape(B, nb, WINDOW, SWA_KV_HEADS, SWA_HEAD_DIM)
        return jnp.concatenate([prev, cur], axis=2)

    kb, vb = band(k), band(v)
    scale = 1.0 / math.sqrt(SWA_HEAD_DIM)
    scores = jnp.einsum('bcihrd,bcjhd->bchrij', qb, kb).astype(jnp.float32) * scale
    i = jnp.arange(WINDOW)[:, None]
    j = jnp.arange(2 * WINDOW)[None, :]
    dist = i + WINDOW - j
    key_pos = jnp.arange(nb)[:, None, None] * WINDOW - WINDOW + j[None]
    valid = (dist >= 0)[None] & (dist < WINDOW)[None] & (key_pos >= 0)
    slopes = jnp.exp2(-8.0 * jnp.arange(1, SWA_HEADS + 1, dtype=jnp.float32) / SWA_HEADS)
    slopes = slopes.reshape(SWA_KV_HEADS, SWA_REP)
    alibi = -slopes[:, :, None, None] * dist.astype(jnp.float32)[None, None]
    scores = jnp.where(valid[None, :, None, None], scores + alibi[None, None], NEG)
    sink = sinks.astype(jnp.float32).reshape(SWA_KV_HEADS, SWA_REP)[None, None, :, :, None, None]
    m = jnp.maximum(jnp.max(scores, axis=-1, keepdims=True), sink)
    p = jnp.exp(scores - m)
    denom = jnp.sum(p, axis=-1, keepdims=True) + jnp.exp(sink - m)
    probs = (p / denom).astype(vb.dtype)
    out = jnp.einsum('bchrij,bcjhd->bcihrd', probs, vb)
    return out.reshape(B, S, SWA_WIDTH)


def mem_branch(q, mk, mv):
    B, S = q.shape[0], q.shape[1]
    scale = 1.0 / math.sqrt(MEM_HEAD_DIM)
    scores = jnp.einsum('bshd,bmhd->bhsm', q, mk).astype(jnp.float32) * scale
    probs = jax.nn.softmax(scores, axis=-1).astype(mv.dtype)
    return jnp.einsum('bhsm,bmhd->bshd', probs, mv).reshape(B, S, MEM_WIDTH)


def setup_inputs(seed: int = 0) -> dict:
    key = jax.random.key(seed)
    ks = jax.random.split(key, 24)
    f32 = jnp.float32
    nrm = lambda k, shape, s: jax.random.normal(k, shape, f32) * s
    gain = lambda k, shape: 1.0 + 0.02 * jax.random.normal(k, shape, f32)
    return {
        "x": nrm(ks[0], (BATCH, SEQ, D_MODEL), 1.0),
        "mem": nrm(ks[1], (BATCH, MEM_LEN, D_MODEL), 1.0),
        "g_ffn1": gain(ks[2], (DEPTH, D_MODEL)),
        "w_ffn1_in": nrm(ks[3], (DEPTH, D_MODEL, 2 * D_FF), D_MODEL ** -0.5),
        "w_ffn1_out": nrm(ks[4], (DEPTH, D_FF, D_MODEL), D_FF ** -0.5),
        "g_mix": gain(ks[5], (DEPTH, D_MODEL)),
        "w_in": nrm(ks[6], (DEPTH, D_MODEL, IN_WIDTH), D_MODEL ** -0.5),
        "gmlp_ln_g": gain(ks[7], (DEPTH, G_WIDTH)),
        "gmlp_ln_b": nrm(ks[8], (DEPTH, G_WIDTH), 0.02),
        "w_s": nrm(ks[9], (DEPTH, G_GROUPS, CHUNK, CHUNK), CHUNK ** -0.5),
        "b_s": gain(ks[10], (DEPTH, G_GROUPS, CHUNK)),
        "swa_sinks": nrm(ks[11], (DEPTH, SWA_HEADS), 0.5),
        "g_mem": gain(ks[12], (DEPTH, D_MODEL)),
        "w_mem_kv": nrm(ks[13], (DEPTH, D_MODEL, 2 * MEM_WIDTH), D_MODEL ** -0.5),
        "w_branch": nrm(ks[14], (DEPTH, N_BRANCH, BRANCH_WIDTH, D_MODEL), BRANCH_WIDTH ** -0.5),
        "w_out": nrm(ks[15], (DEPTH, D_MODEL, D_MODEL), D_MODEL ** -0.5),
        "g_ffn2": gain(ks[16], (DEPTH, D_MODEL)),
        "w_ffn2_in": nrm(ks[17], (DEPTH, D_MODEL, 2 * D_FF), D_MODEL ** -0.5),
        "w_ffn2_out": nrm(ks[18], (DEPTH, D_FF, D_MODEL), D_FF ** -0.5),
        "g_final": gain(ks[19], (D_MODEL,)),
    }


def reference(x, mem, g_ffn1, w_ffn1_in, w_ffn1_out, g_mix, w_in, gmlp_ln_g, gmlp_ln_b,
              w_s, b_s, swa_sinks, g_mem, w_mem_kv, w_branch, w_out, g_ffn2, w_ffn2_in,
              w_ffn2_out, g_final):
    B, S, _ = x.shape
    split_at = list(np.cumsum([G_WIDTH, G_WIDTH, SWA_WIDTH, SWA_KV_WIDTH, SWA_KV_WIDTH, MEM_WIDTH]))
    for l in range(DEPTH):
        h = x + 0.5 * swiglu(rmsnorm(x, g_ffn1[l]), w_ffn1_in[l], w_ffn1_out[l])
        n = rmsnorm(h, g_mix[l])
        z = n @ w_in[l]
        z_u, z_v, z_q, z_k, z_vv, z_mq, z_gate = jnp.split(z, split_at, axis=-1)
        o_a = gmlp_branch(jax.nn.gelu(z_u, approximate=False), jax.nn.gelu(z_v, approximate=False),
                          w_s[l], b_s[l], gmlp_ln_g[l], gmlp_ln_b[l])
        o_b = swa_branch(z_q.reshape(B, S, SWA_HEADS, SWA_HEAD_DIM),
                         z_k.reshape(B, S, SWA_KV_HEADS, SWA_HEAD_DIM),
                         z_vv.reshape(B, S, SWA_KV_HEADS, SWA_HEAD_DIM),
                         swa_sinks[l])
        mkv = rmsnorm(mem, g_mem[l]) @ w_mem_kv[l]
        mk, mv = jnp.split(mkv, 2, axis=-1)
        o_c = mem_branch(z_mq.reshape(B, S, MEM_HEADS, MEM_HEAD_DIM),
                         mk.reshape(B, MEM_LEN, MEM_HEADS, MEM_HEAD_DIM),
                         mv.reshape(B, MEM_LEN, MEM_HEADS, MEM_HEAD_DIM))
        gates = jax.nn.sigmoid(z_gate.reshape(B, S, N_BRANCH, D_MODEL))
        y = (gates[:, :, 0] * (o_a @ w_branch[l, 0])
             + gates[:, :, 1] * (o_b @ w_branch[l, 1])
             + gates[:, :, 2] * (o_c @ w_branch[l, 2]))
        h = h + y @ w_out[l]
        x = h + 0.5 * swiglu(rmsnorm(h, g_ffn2[l]), w_ffn2_in[l], w_ffn2_out[l])
    return rmsnorm(x, g_final)
```

```python
import contextlib
from functools import partial
import numpy as np
import concourse.bass as bass
import concourse.mybir as mybir
from concourse.bass_utils import run_bass_kernel_spmd

F32 = mybir.dt.float32
BF16 = mybir.dt.bfloat16
AF = mybir.ActivationFunctionType
ALU = mybir.AluOpType
AX = mybir.AxisListType
ENGS = ("pe", "act", "dve", "pool", "sp")

D = 2048
DFF = 5504
NF = 43
INW = 10752
TT = 768
NS = 384
NBLK = 6
WIN = 2304
EPS = 1e-6
BIG = 1.0e6


class Sched:
    strict_same = True
    nwaits = None

    def __init__(self, nc, same_eng_sync=True):
        self.nc = nc
        self.ops = []
        self.res = {}
        self.dma_cnt = {}
        self.same_eng_sync = same_eng_sync

    def op(self, eng, fn, reads=(), writes=(), dma=None, ndma=1):
        i = len(self.ops)
        deps = {}
        for k in reads:
            r = self.res.setdefault(k, [None, []])
            if r[0] is not None:
                deps[r[0]] = "raw"
            r[1].append(i)
        for k in writes:
            r = self.res.setdefault(k, [None, []])
            if r[0] is not None and r[0] != i:
                deps.setdefault(r[0], "waw")
            for x in r[1]:
                if x != i:
                    deps.setdefault(x, "war")
            r[0] = i
            r[1] = []
        o = dict(eng=eng, fn=fn, deps=deps, dma=dma, sig=False, idx=None, cnt=None)
        if dma is not None:
            self.dma_cnt[dma] = self.dma_cnt.get(dma, 0) + 16 * ndma
            o["cnt"] = self.dma_cnt[dma]
        self.ops.append(o)
        return i

    def switch(self, old_keys, new_keys):
        ids = set()
        for k in old_keys:
            r = self.res.pop(k, None)
            if r is None:
                continue
            if r[0] is not None:
                ids.add(r[0])
            ids.update(r[1])
        best = {}
        for i in ids:
            o = self.ops[i]
            key = ("d", o["dma"]) if o["dma"] is not None else ("e", o["eng"])
            if key not in best or best[key] < i:
                best[key] = i
        lst = list(best.values())
        for k in new_keys:
            r = self.res.setdefault(k, [None, []])
            r[1].extend(lst)

    def emit(self, final_waits=()):
        nc = self.nc
        ops = self.ops
        for o in ops:
            need = {}
            for d, kind in o["deps"].items():
                Dd = ops[d]
                if Dd["dma"] is not None:
                    need[d] = kind
                    continue
                if Dd["eng"] == o["eng"] and o["dma"] is None:
                    if o["eng"] == "pe":
                        continue
                    if not self.same_eng_sync or (kind != "raw" and not self.strict_same):
                        continue
                Dd["sig"] = True
                need[d] = kind
            o["need"] = need
        cnt = {e: 0 for e in ENGS}
        for o in ops:
            if o["sig"]:
                cnt[o["eng"]] += 1
                o["idx"] = cnt[o["eng"]]
        with contextlib.ExitStack() as es:
            esem = {e: es.enter_context(nc.semaphore("s_" + e)) for e in ENGS}
            dsem = {}
            for n, k in enumerate(self.dma_cnt):
                dsem[k] = es.enter_context(nc.semaphore("d%d" % n))
            block = es.enter_context(nc.Block())

            def run(engname):
                def body(eng):
                    seen = {}
                    for o in ops:
                        if o["eng"] != engname:
                            continue
                        for d in o["need"]:
                            Dd = ops[d]
                            if Dd["dma"] is not None:
                                s, v, sk = dsem[Dd["dma"]], Dd["cnt"], ("d", Dd["dma"])
                            else:
                                s, v, sk = esem[Dd["eng"]], Dd["idx"], ("e", Dd["eng"])
                            if seen.get(sk, 0) >= v:
                                continue
                            seen[sk] = v
                            if Sched.nwaits is not None:
                                Sched.nwaits[(engname, sk[1] if sk[0] == "e" else "dma")] = Sched.nwaits.get((engname, sk[1] if sk[0] == "e" else "dma"), 0) + 1
                            eng.wait_ge(s, v)
                        ins = o["fn"](eng)
                        if o["dma"] is not None:
                            if not isinstance(ins, (list, tuple)):
                                ins = [ins]
                            for x in ins:
                                x.then_inc(dsem[o["dma"]], 16)
                        elif o["sig"]:
                            ins.then_inc(esem[engname], 1)
                    if engname == "sp":
                        for k in [k for k in final_waits if k in dsem]:
                            eng.wait_ge(dsem[k], self.dma_cnt[k])
                return body

            block.tensor(run("pe"))
            block.scalar(run("act"))
            block.vector(run("dve"))
            block.gpsimd(run("pool"))
            block.sync(run("sp"))


def _mm(out, lhsT, rhs, st, sp, e):
    return e.matmul(out, lhsT, rhs, start=st, stop=sp)


def _dma(pairs, e):
    return [e.dma_start(out=o, in_=i) for o, i in pairs]


def _act(out, in_, func, scale, e):
    return e.activation(out, in_, func, scale=scale)


def _actb(out, in_, func, bias, scale, e):
    return e.activation(out, in_, func, bias=bias, scale=scale)


def _tt(out, a, b, op, e):
    return e.tensor_tensor(out, a, b, op)


def _ts(out, a, s1, s2, op0, op1, e):
    return e.tensor_scalar(out, a, s1, s2, op0, op1)


def _ts1(out, a, s1, op0, e):
    return e.tensor_single_scalar(out, a, s1, op0)


def _stt(out, a, s, b, op0, op1, e):
    return e.scalar_tensor_tensor(out, a, s, b, op0, op1)


def _acopy(out, a, e):
    return e.activation(out, a, AF.Identity)


def _copy(out, a, e):
    return e.tensor_copy(out, a)


def _rsum(out, a, e):
    return e.reduce_sum(out, a, AX.X)


def _recip(out, a, e):
    return e.reciprocal(out, a)


def _memset(ap, v, e):
    return e.memset(ap, v)


def build_nc(n_tiles=3, depth=2, dbg_names=(), skip=()):
    nc = bass.Bass("TRN2", target_bir_lowering=False)
    es = contextlib.ExitStack()
    es.enter_context(nc.allow_low_precision("bf16 matmul operands, fp32 accumulate"))

    def din(name, shape):
        return nc.dram_tensor(name, list(shape), F32, kind="ExternalInput").ap()

    xT_d = din("xT", [D, WIN])
    memT_d = din("memT", [D, 256])
    w_ffn_in = [din("w_ffn1_in", [2, D, 2 * DFF]), din("w_ffn2_in", [2, D, 2 * DFF])]
    w_ffn_out = [din("w_ffn1_out", [2, DFF, D]), din("w_ffn2_out", [2, DFF, D])]
    w_in_d = din("w_in", [2, D, INW])
    w_memkv_d = din("w_mem_kv", [2, D, 2048])
    w_branch_d = din("w_branch", [2, 3, 1024, D])
    w_out_d = din("w_out", [2, D, D])
    gv_d = din("gv", [128, 144])
    lnp_d = din("lnp", [2, 128, 3072])
    wsT_d = din("wsT", [2, 128, 512])
    maskT_d = din("maskT", [128, 512])
    sinks_d = din("sinks", [2, 128, 16])
    dist_d = din("dist", [2, 128, 256])
    flag_d = din("flag", [128, 1])
    outT_d = nc.dram_tensor("outT", [D, WIN - 256], F32, kind="ExternalOutput").ap()
    dbg_d = {n: nc.dram_tensor("dbg_" + n, list(shp), dt, kind="ExternalOutput").ap() for n, shp, dt in dbg_names}

    mkv_s = nc.dram_tensor("mkv_scratch", [2, 128, 4096], BF16).ap()
    S = Sched(nc)
    sb = lambda n, shp, dt: nc.alloc_sbuf_tensor("sb_" + n, shp, dt)

    xT = sb("xT", [128, 16, TT], F32)
    nT = sb("nT", [128, 16, TT], BF16)
    AR = sb("arena", [128, 32 * TT], BF16)
    A0 = AR[:, 0:8 * TT].rearrange("p (c t) -> p c t", c=8)
    A1 = AR[:, 8 * TT:16 * TT].rearrange("p (c t) -> p c t", c=8)
    A1m = AR[:, 8 * TT:8 * TT + 16 * 256].rearrange("p (c t) -> p c t", c=16)
    A2 = AR[:, 16 * TT:24 * TT].rearrange("p (c t) -> p c t", c=8)
    A3 = AR[:, 24 * TT:32 * TT]
    A3vn = A3.rearrange("p (j f) -> p j f", j=NBLK)
    A3q = A3.rearrange("p (c t) -> p c t", c=8)
    A3mk = A3[:, 0:2048].rearrange("p (c m) -> p c m", c=8)
    A3mv = A3[:, 2048:4096].rearrange("p (c f) -> p c f", c=2)
    gT = AR.rearrange("p (f t) -> p f t", f=32)
    R2 = sb("r2", [128, 4608], F32)
    lng, lnb = R2[:, 0:1024], R2[:, 1024:2048]
    bsb = R2[:, 2048:3072].rearrange("p (c t) -> p c t", c=8)
    R2b = R2[:].bitcast(BF16)
    kEO = R2b[:, 0:8 * TT].rearrange("p (e g t) -> p e g t", e=2, g=4)
    Vt = R2b[:, 8 * TT:8 * TT + NBLK * 512].rearrange("p (j g f) -> p j g f", j=NBLK, g=4)
    carK = sb("carK", [128, 2, 2, 4, 128], BF16)
    carV = sb("carV", [128, 2, 4, 128], BF16)
    wsT = sb("wsT", [128, 512], BF16)
    wsF = sb("wsF", [128, 512], F32)
    maskT = sb("maskT", [128, 512], F32)
    Etab = sb("Etab", [128, 16, 256], BF16)
    flag = sb("flag", [128, 1], F32)
    gv = sb("gv", [128, 144], F32)
    es_t = sb("es", [128, 16], F32)
    st = sb("st", [128, 8], F32)
    ones_m = sb("ones_m", [128, 128], BF16)
    epsT = sb("epsT", [128, 1], F32)
    ones_b = sb("ones_b", [128, 128], BF16)
    NSLOT = 3
    wslot = [sb("wslot%d" % i, [128, 4096], BF16) for i in range(NSLOT)]
    SC = [sb("sc%d" % i, [128, 512], F32) for i in range(4)]
    SQ = [sb("sq%d" % i, [128, NS], BF16) for i in range(4)]
    ACC = sb("acc", [128, 2, NS], F32)
    SF = [sb("sf%d" % i, [128, 1024], F32) for i in range(1)]
    SB = [sb("sbb%d" % i, [128, 1024], BF16) for i in range(2)]
    RS = [sb("rs%d" % i, [128, NS], F32) for i in range(2)]
    PS = [nc.alloc_psum_tensor("ps%d" % i, [128, 512], F32) for i in range(8)]

    ctr = dict(w=0, ps=0, sc=0, sf=0, sb=0, ms=0, rs=0, sq=0)

    cur = dict(lo=0)

    def TS(s):
        return slice(cur["lo"] * 128 if s == 0 else NS, (s + 1) * NS)

    def need(ti, l, phase):
        if ti > 0:
            return 0
        return {"ffn1": l, "kv": l, "mix": l + 1, "ffn2": l + 1, "final": depth}[phase]

    def ring(name, n):
        i = ctr[name] % n
        ctr[name] += 1
        return i

    def psum():
        i = ring("ps", 6)
        return PS[i], ("ps", i)

    STAT = [(PS[6], ("ps", 6)), (PS[7], ("ps", 7))]
    fused = dict(ready=False, pend=[])

    def stat_push(c, s, last=False):
        ts = TS(s)
        qi = ring("sq", 4)
        sq, sk = SQ[qi], ("sq", qi)
        S.op("dve", partial(_tt, sq[:, 0:ts.stop - ts.start], xT[:, c, ts], xT[:, c, ts], ALU.mult), reads=[("x", c)], writes=[sk])
        fused["pend"].append((c, s, sq, sk, ts.stop - ts.start))
        while len(fused["pend"]) > (0 if last else 3):
            c0, s0, sq0, sk0, n0 = fused["pend"].pop(0)
            S.op("pe", partial(_mm, STAT[s0][0][:, 0:n0], ones_m[:], sq0[:, 0:n0], c0 == 0, c0 == 15), reads=[sk0, "ones_m"], writes=[STAT[s0][1]])
        if last:
            fused["ready"] = True

    def scr():
        i = ring("sc", 4)
        return SC[i], ("sc", i)

    def scrF():
        i = ring("sf", 1)
        return SF[i], ("sf", i)

    def scrB():
        i = ring("sb", 2)
        return SB[i], ("sb", i)

    def wload(pairs_fn):
        i = ring("w", NSLOT)
        pairs = pairs_fn(wslot[i])
        S.op("pool", partial(_dma, pairs), writes=[("w", i)], dma=("w", i), ndma=len(pairs))
        return wslot[i], ("w", i)

    def wview(ap2d):
        return ap2d.rearrange("(k p) c -> p k c", p=128)

    S.op("sp", partial(_dma, [(gv[:], gv_d), (maskT[:], maskT_d), (flag[:], flag_d)]),
         writes=["gv", "maskT", "flag"], dma="const", ndma=3)
    dtmp, dtk = SC[0], ("sc", 0)
    S.op("sp", partial(_dma, [(dtmp[:, 0:256], dist_d[0])]), writes=[dtk], dma="const2")
    for h in range(16):
        S.op("act", partial(_act, Etab[:, h, :], dtmp[:, 0:256], AF.Exp, -(2.0 ** (-8.0 * (h + 1) / 16.0))), reads=[dtk], writes=["Etab"])
    S.op("dve", partial(_memset, ones_m[:], 1.0 / D), writes=["ones_m"])
    S.op("dve", partial(_memset, ones_b[:], 1.0), writes=["ones_b"])
    S.op("dve", partial(_memset, epsT[:], EPS), writes=["epsT"])
    S.op("dve", partial(_memset, carK[:], 0.0), writes=[("carK", 0), ("carK", 1)])
    S.op("dve", partial(_memset, carV[:], 0.0), writes=[("carV", 0), ("carV", 1)])

    def dbg(name, ap, reads):
        if name in dbg_d:
            S.op("sp", partial(_dma, [(dbg_d[name], ap)]), reads=reads, dma="dbg_" + name)

    XK = [("x", c) for c in range(16)]
    NK = [("n", c) for c in range(16)]

    def rmsnorm(gcol, out_fn=None):
        use_fused = fused["ready"]
        fused["ready"] = False
        for s in range(2):
            ts = TS(s)
            if use_fused:
                pst, pk = STAT[s]
            else:
                pst, pk = psum()
                for c in range(16):
                    qi = ring("sq", 4)
                    sq, sk = SQ[qi], ("sq", qi)
                    S.op("dve", partial(_tt, sq[:, 0:ts.stop - ts.start], xT[:, c, ts], xT[:, c, ts], ALU.mult), reads=[("x", c)], writes=[sk])
                    S.op("pe", partial(_mm, pst[:, 0:ts.stop - ts.start], ones_m[:], sq[:, 0:ts.stop - ts.start], c == 0, c == 15), reads=[sk, "ones_m"], writes=[pk])
            ri = ring("rs", 2)
            rs, rk = RS[ri], ("rs", ri)
            S.op("act", partial(_actb, rs[:, 0:ts.stop - ts.start], pst[:, 0:ts.stop - ts.start], AF.Sqrt, epsT[:], 1.0), reads=[pk, "epsT"], writes=[rk])
            S.op("dve", partial(_recip, rs[:, 0:ts.stop - ts.start], rs[:, 0:ts.stop - ts.start]), reads=[rk], writes=[rk])
            for c in range(16):
                if out_fn is None:
                    S.op("dve", partial(_stt, nT[:, c, ts], xT[:, c, ts], gv[:, gcol + c:gcol + c + 1], rs[:, 0:ts.stop - ts.start], ALU.mult, ALU.mult),
                         reads=[("x", c), rk, "gv"], writes=[("n", c)])
                else:
                    out_fn(c, s, ts, rs, rk)

    def ffn(w_in_l, w_out_l, gcol):
        rmsnorm(gcol)
        wi = wview(w_in_l)
        wo = w_out_l.rearrange("(f p) c -> p f c", p=128)
        for (f0, f1) in ((0, 22), (22, 43)):
            nf = f1 - f0
            for f in range(f0, f1):
                def pf(t, f=f):
                    v = t[:, :].rearrange("p (k a c) -> p k a c", k=16, a=2)
                    return [(v[:, :, 0, :], wi[:, :, f * 128:(f + 1) * 128]),
                            (v[:, :, 1, :], wi[:, :, DFF + f * 128:DFF + (f + 1) * 128])]
                wt, wk = wload(pf)
                wv = wt[:, :].rearrange("p (k a c) -> p k a c", k=16, a=2)
                for s in range(2):
                    ts = TS(s)
                    pa, pak = psum()
                    pb, pbk = psum()
                    for k in range(16):
                        S.op("pe", partial(_mm, pa[:, 0:ts.stop - ts.start], wv[:, k, 0, :], nT[:, k, ts], k == 0, k == 15), reads=[wk, ("n", k)], writes=[pak])
                    for k in range(16):
                        S.op("pe", partial(_mm, pb[:, 0:ts.stop - ts.start], wv[:, k, 1, :], nT[:, k, ts], k == 0, k == 15), reads=[wk, ("n", k)], writes=[pbk])
                    sg, sgk = scr()
                    S.op("act", partial(_act, sg[:, 0:ts.stop - ts.start], pa[:, 0:ts.stop - ts.start], AF.Silu, 1.0), reads=[pak], writes=[sgk])
                    S.op("dve", partial(_tt, gT[:, f - f0, ts], sg[:, 0:ts.stop - ts.start], pb[:, 0:ts.stop - ts.start], ALU.mult), reads=[sgk, pbk], writes=[("g", f - f0)])
            for d in range(16):
                def pf(t, d=d):
                    v = t[:, 0:nf * 128].rearrange("p (f c) -> p f c", f=nf)
                    return [(v, wo[:, f0:f1, d * 128:(d + 1) * 128])]
                wt, wk = wload(pf)
                wv = wt[:, 0:nf * 128].rearrange("p (f c) -> p f c", f=nf)
                for s in range(2):
                    ts = TS(s)
                    po, pok = psum()
                    for f in range(nf):
                        S.op("pe", partial(_mm, po[:, 0:ts.stop - ts.start], wv[:, f, :], gT[:, f, ts], f == 0, f == nf - 1), reads=[wk, ("g", f)], writes=[pok])
                    S.op("dve", partial(_stt, xT[:, d, ts], po[:, 0:ts.stop - ts.start], 0.5, xT[:, d, ts], ALU.mult, ALU.add), reads=[pok, ("x", d)], writes=[("x", d)])
                    if f0 > 0:
                        stat_push(d, s, last=(d == 15 and s == 1))

    GK = [("g", f) for f in range(22)]
    MIXK = ([("u", c) for c in range(8)] + [("ob", h) for h in range(8)] + ["memn"] + [("mq", c) for c in range(8)]
            + [("vn", j) for j in range(NBLK)] + [("q", c) for c in range(8)] + ["mk", "mv"] + [("y", c) for c in range(8)])

    def proj_chunk(wcols_fn, evac, c):
        def pf(t):
            return wcols_fn(c, t[:, 0:2048].rearrange("p (k c) -> p k c", k=16))
        wt, wk = wload(pf)
        wv = wt[:, 0:2048].rearrange("p (k c) -> p k c", k=16)
        for s in range(2):
            ts = TS(s)
            ps, pk = psum()
            for k in range(16):
                S.op("pe", partial(_mm, ps[:, 0:ts.stop - ts.start], wv[:, k, :], nT[:, k, ts], k == 0, k == 15), reads=[wk, ("n", k)], writes=[pk])
            evac(c, s, ts, ps, pk)

    def proj_fm(wcols_fn, nchunks, evac):
        for c in range(nchunks):
            proj_chunk(wcols_fn, evac, c)

    def mixer(l, tile_i):
        wi = wview(w_in_d[l])
        gcol = (l * 4 + 1) * 16
        kvlo, mixlo = need(tile_i, l, "kv"), need(tile_i, l, "mix")
        cur["lo"] = kvlo
        rmsnorm(gcol)
        cur["lo"] = mixlo
        S.op("sp", partial(_dma, [(es_t[:], sinks_d[l])]), writes=["es"], dma="tab_es")
        S.op("act", partial(_act, es_t[:], es_t[:], AF.Exp, 1.0), reads=["es"], writes=["es"])
        S.switch(["kTd", "Vt"], ["lnp"])
        S.op("sp", partial(_dma, [(R2[:, 0:3072], lnp_d[l]), (wsF[:], wsT_d[l])]), writes=["lnp", "wsF"], dma="tab_ln", ndma=2)
        S.op("dve", partial(_tt, wsT[:], wsF[:], maskT[:], ALU.mult), reads=["wsF", "maskT"], writes=["wsT"])

        for cg in range(4):
            def pf(t, cg=cg):
                return [(t[:, :].rearrange("p (k c) -> p k c", k=16), wi[:, :, 1024 + cg * 256:1024 + (cg + 1) * 256])]
            wt, wk = wload(pf)
            wv = wt[:, :].rearrange("p (k c) -> p k c", k=16)
            for j in range(mixlo, NBLK):
                ps, pk = psum()
                for k in range(16):
                    S.op("pe", partial(_mm, ps[:, 0:256], nT[:, k, j * 128:(j + 1) * 128], wv[:, k, :], k == 0, k == 15), reads=[wk, ("n", k)], writes=[pk])
                S.op("act", partial(_act, A3vn[:, j, cg * 256:(cg + 1) * 256], ps[:, 0:256], AF.Gelu, 1.0), reads=[pk], writes=[("vn", j)])

        def memnorm_emit():
            mgcol = (l * 4 + 3) * 16
            memv = memT_d.rearrange("(c p) m -> p c m", p=128)
            pst, pk = psum()
            for c in range(16):
                msb, msk = scr()
                S.op("sp", partial(_dma, [(msb[:, 0:256], memv[:, c, :])]), writes=[msk], dma=("ms",) + msk)
                qi = ring("sq", 4)
                sq, sk = SQ[qi], ("sq", qi)
                S.op("dve", partial(_tt, sq[:, 0:256], msb[:, 0:256], msb[:, 0:256], ALU.mult), reads=[msk], writes=[sk])
                S.op("pe", partial(_mm, pst[:, 0:256], ones_m[:], sq[:, 0:256], c == 0, c == 15), reads=[sk, "ones_m"], writes=[pk])
            ri = ring("rs", 2)
            rs, rk = RS[ri], ("rs", ri)
            S.op("act", partial(_actb, rs[:, 0:256], pst[:, 0:256], AF.Sqrt, epsT[:], 1.0), reads=[pk, "epsT"], writes=[rk])
            S.op("dve", partial(_recip, rs[:, 0:256], rs[:, 0:256]), reads=[rk], writes=[rk])
            for c in range(16):
                msb, msk = scr()
                S.op("sp", partial(_dma, [(msb[:, 0:256], memv[:, c, :])]), writes=[msk], dma=("ms",) + msk)
                S.op("dve", partial(_stt, A1m[:, c, :], msb[:, 0:256], gv[:, mgcol + c:mgcol + c + 1], rs[:, 0:256], ALU.mult, ALU.mult),
                     reads=[msk, rk, "gv"], writes=["memn"])


        def u_w(c, v):
            return [(v, wi[:, :, c * 128:(c + 1) * 128])]

        def u_ev(c, s, ts, ps, pk):
            S.op("act", partial(_act, A0[:, c, ts], ps[:, 0:ts.stop - ts.start], AF.Gelu, 1.0), reads=[pk], writes=[("u", c)])

        def mq_w(c, v):
            return [(v, wi[:, :, 3584 + c * 128:3584 + (c + 1) * 128])]

        def mq_ev(c, s, ts, ps, pk):
            S.op("act", partial(_acopy, A2[:, c, ts], ps[:, 0:ts.stop - ts.start]), reads=[pk], writes=[("mq", c)])
        pchunks = [partial(proj_chunk, u_w, u_ev, c) for c in range(8)] + [partial(proj_chunk, mq_w, mq_ev, c) for c in range(8)]

        def ln_blk(j):
            vj = A3vn[:, j, :]
            vk = ("vn", j)
            tf, tfk = scrF()
            S.op("dve", partial(_rsum, st[:, 0:1], vj), reads=[vk], writes=["st0"])
            S.op("dve", partial(_tt, tf[:], vj, vj, ALU.mult), reads=[vk], writes=[tfk])
            S.op("dve", partial(_rsum, st[:, 1:2], tf[:]), reads=[tfk], writes=["st1"])
            S.op("dve", partial(_ts1, st[:, 2:3], st[:, 0:1], 1.0 / 1024, ALU.mult), reads=["st0"], writes=["st2"])
            S.op("dve", partial(_tt, st[:, 3:4], st[:, 2:3], st[:, 2:3], ALU.mult), reads=["st2"], writes=["st3"])
            S.op("dve", partial(_stt, st[:, 4:5], st[:, 1:2], 1.0 / 1024, st[:, 3:4], ALU.mult, ALU.subtract), reads=["st1", "st3"], writes=["st4"])
            S.op("act", partial(_actb, st[:, 5:6], st[:, 4:5], AF.Sqrt, epsT[:], 1.0), reads=["st4", "epsT"], writes=["st5"])
            S.op("dve", partial(_recip, st[:, 5:6], st[:, 5:6]), reads=["st5"], writes=["st5"])
            S.op("dve", partial(_stt, st[:, 6:7], st[:, 2:3], -1.0, st[:, 5:6], ALU.mult, ALU.mult), reads=["st2", "st5"], writes=["st6"])
            S.op("act", partial(_actb, tf[:], vj, AF.Identity, st[:, 6:7], st[:, 5:6]), reads=[vk, "st5", "st6", tfk], writes=[tfk])
            S.op("dve", partial(_tt, tf[:], tf[:], lng, ALU.mult), reads=[tfk, "lnp"], writes=[tfk])
            S.op("dve", partial(_tt, vj, tf[:], lnb, ALU.add), reads=[tfk, "lnp"], writes=[vk])

        def mix_blk(j):
            vk = ("vn", j)
            for hc in range(2):
                ps, pk = psum()
                for cc in range(4):
                    c = hc * 4 + cc
                    g = c // 2
                    S.op("pe", partial(_mm, ps[:, cc * 128:(cc + 1) * 128], A3vn[:, j, c * 128:(c + 1) * 128], wsT[:, g * 128:(g + 1) * 128], True, True),
                         reads=[vk, "wsT"], writes=[pk])
                t2, t2k = scr()
                S.op("dve", partial(_tt, t2[:].rearrange("p (c t) -> p c t", c=4), ps[:].rearrange("p (c t) -> p c t", c=4), bsb[:, hc * 4:(hc + 1) * 4, :], ALU.add),
                     reads=[pk, "lnp"], writes=[t2k])
                uo = A0[:, hc * 4:(hc + 1) * 4, j * 128:(j + 1) * 128]
                S.op("dve", partial(_tt, uo, t2[:].rearrange("p (c t) -> p c t", c=4), uo, ALU.mult),
                     reads=[t2k] + [("u", hc * 4 + cc) for cc in range(4)], writes=[("u", hc * 4 + cc) for cc in range(4)])

        blks = list(range(mixlo, NBLK))
        nmix = 0
        for bi, j in enumerate(blks):
            ln_blk(j)
            ntake = -(-len(pchunks) // (len(blks) - bi))
            for _ in range(ntake):
                pchunks.pop(0)()
            if bi == 0 and tile_i == 0:
                memnorm_emit()
            if len(pchunks) <= 8:
                while nmix < bi:
                    mix_blk(blks[nmix])
                    nmix += 1
        while nmix < len(blks):
            mix_blk(blks[nmix])
            nmix += 1
        assert not pchunks
        dbg("oa", A0[:, 0, 128:256], [("u", 0)])

        S.switch([("vn", j) for j in range(NBLK)], ["mk", "mv"])
        S.switch(["lnp"], ["kTd", "Vt"])
        if tile_i == 0:
            wm = wview(w_memkv_d[l])
            for c in range(8):
                def pf(t, c=c):
                    return [(t[:, 0:2048].rearrange("p (k c) -> p k c", k=16), wm[:, :, c * 128:(c + 1) * 128])]
                wt, wk = wload(pf)
                wv = wt[:, 0:2048].rearrange("p (k c) -> p k c", k=16)
                ps, pk = psum()
                for k in range(16):
                    S.op("pe", partial(_mm, ps[:, 0:256], wv[:, k, :], A1m[:, k, :], k == 0, k == 15), reads=[wk, "memn"], writes=[pk])
                S.op("act", partial(_acopy, A3mk[:, c, :], ps[:, 0:256]), reads=[pk], writes=["mk"])
            for cg in range(4):
                def pf(t, cg=cg):
                    return [(t[:, :].rearrange("p (k c) -> p k c", k=16), wm[:, :, 1024 + cg * 256:1024 + (cg + 1) * 256])]
                wt, wk = wload(pf)
                wv = wt[:, :].rearrange("p (k c) -> p k c", k=16)
                for mc in range(2):
                    ps, pk = psum()
                    for k in range(16):
                        S.op("pe", partial(_mm, ps[:, 0:256], A1m[:, k, mc * 128:(mc + 1) * 128], wv[:, k, :], k == 0, k == 15), reads=[wk, "memn"], writes=[pk])
                    S.op("act", partial(_acopy, A3mv[:, mc, cg * 256:(cg + 1) * 256], ps[:, 0:256]), reads=[pk], writes=["mv"])
            S.op("sp", partial(_dma, [(mkv_s[l], A3[:, 0:4096])]), reads=["mk", "mv"], writes=[("mkvs", l)], dma=("mkvs_w", l))
        else:
            S.op("sp", partial(_dma, [(A3[:, 0:4096], mkv_s[l])]), reads=[("mkvs", l)], writes=["mk", "mv"], dma=("mkvs_r", l))

        def k_w(g, v):
            src = wi[:, :, 3072 + g * 64:3072 + (g + 1) * 64]
            return [(v[:, :, 0:64], src), (v[:, :, 64:128], src)]

        S.op("dve", partial(_memset, kEO[64:128, 0, :, :], 0.0), writes=["kTd"])
        S.op("dve", partial(_memset, kEO[0:64, 1, :, :], 0.0), writes=["kTd"])

        def k_ev(g, s, ts, ps, pk):
            S.op("act", partial(_acopy, kEO[0:64, 0, g, ts], ps[0:64, 0:ts.stop - ts.start]), reads=[pk], writes=["kTd"])
            S.op("dve", partial(_copy, kEO[64:128, 1, g, ts], ps[64:128, 0:ts.stop - ts.start]), reads=[pk], writes=["kTd"])
        cur["lo"] = kvlo
        proj_fm(k_w, 4, k_ev)

        def pf(t):
            return [(t[:, :].rearrange("p (k c) -> p k c", k=16), wi[:, :, 3328:3584])]
        wt, wk = wload(pf)
        wv = wt[:, :].rearrange("p (k c) -> p k c", k=16)
        for j in range(kvlo, NBLK):
            ps, pk = psum()
            for k in range(16):
                S.op("pe", partial(_mm, ps[:, 0:256], nT[:, k, j * 128:(j + 1) * 128], wv[:, k, :], k == 0, k == 15), reads=[wk, ("n", k)], writes=[pk])
            psv = ps[:, 0:256].rearrange("p (g d) -> p g d", g=4)
            S.op("act", partial(_acopy, Vt[:, j, :, 0:64], psv), reads=[pk], writes=["Vt"])
            S.op("dve", partial(_copy, Vt[:, j, :, 64:128], psv), reads=[pk], writes=["Vt"])

        cur["lo"] = mixlo
        def m1(s, h):
            ts = TS(s)
            pt, ptk = scrB()
            ptv = pt[:, 0:2 * NS].rearrange("p (m t) -> p m t", m=2)[:, :, 0:ts.stop - ts.start]
            for mc in range(2):
                ps, pk = psum()
                for dc in range(2):
                    S.op("pe", partial(_mm, ps[:, 0:ts.stop - ts.start], A3mk[:, 2 * h + dc, mc * 128:(mc + 1) * 128], A2[:, 2 * h + dc, ts], dc == 0, dc == 1),
                         reads=["mk", ("mq", 2 * h + dc)], writes=[pk])
                S.op("act", partial(_act, ptv[:, mc, :], ps[:, 0:ts.stop - ts.start], AF.Exp, 1.0 / 16.0), reads=[pk], writes=[ptk])
            return ptv, ptk

        def m2(s, h, ptv, ptk):
            ts = TS(s)
            pd, pdk = psum()
            for mc in range(2):
                S.op("pe", partial(_mm, pd[:, 0:ts.stop - ts.start], ones_b[:], ptv[:, mc, :], mc == 0, mc == 1), reads=[ptk, "ones_b"], writes=[pdk])
            rd, rdk = scr()
            S.op("dve", partial(_recip, rd[:, 0:ts.stop - ts.start], pd[:, 0:ts.stop - ts.start]), reads=[pdk], writes=[rdk])
            for dc in range(2):
                po, pok = psum()
                for mc in range(2):
                    S.op("pe", partial(_mm, po[:, 0:ts.stop - ts.start], A3mv[:, mc, (2 * h + dc) * 128:(2 * h + dc + 1) * 128], ptv[:, mc, :], mc == 0, mc == 1),
                         reads=["mv", ptk], writes=[pok])
                S.op("dve", partial(_tt, A2[:, 2 * h + dc, ts], po[:, 0:ts.stop - ts.start], rd[:, 0:ts.stop - ts.start], ALU.mult), reads=[pok, rdk], writes=[("mq", 2 * h + dc)])

        items = [(s, h) for s in range(2) for h in range(4)]
        prev = None
        for it in items:
            ctx = m1(*it)
            if prev is not None:
                m2(*prev)
            prev = it + ctx
        m2(*prev)
        dbg("oc", A2[:, 0, 128:256], [("mq", 0)])

        S.switch(["mk", "mv"], [("q", c) for c in range(8)])
        S.switch(["memn"], [("ob", h) for h in range(8)])

        def q_w(c, v):
            return [(v, wi[:, :, 2048 + c * 128:2048 + (c + 1) * 128])]

        def q_ev(c, s, ts, ps, pk):
            S.op("act", partial(_acopy, A3q[:, c, ts], ps[:, 0:ts.stop - ts.start]), reads=[pk], writes=[("q", c)])
        proj_fm(q_w, 8, q_ev)

        def sw1(j, g):
            first = (tile_i == 0 and j == 2)
            bt = slice(j * 128, (j + 1) * 128)
            pbank = [psum(), psum()]
            for r in range(4):
                h = 4 * g + r
                c, ph = h // 2, h % 2
                pq, pqk = pbank[r // 2]
                for kc in range(2):
                    if kc == 0:
                        if j == 0:
                            kk, kkey = carK[:, l, ph, g, :], ("carK", l)
                        else:
                            kk, kkey = kEO[:, ph, g, (j - 1) * 128:j * 128], "kTd"
                    else:
                        kk, kkey = kEO[:, ph, g, bt], "kTd"
                    o = (r % 2) * 256 + kc * 128
                    S.op("pe", partial(_mm, pq[:, o:o + 128], kk, A3q[:, c, bt], True, True), reads=[kkey, ("q", c)], writes=[pqk])
            pt, ptk = scrB()
            for bnk in range(2):
                pq, pqk = pbank[bnk]
                S.op("act", partial(_act, pt[:, bnk * 512:(bnk + 1) * 512], pq[:, :], AF.Exp, 0.125), reads=[pqk], writes=[ptk])
            S.op("dve", partial(_tt, pt[:], pt[:], Etab[:, 4 * g:4 * g + 4, :].rearrange("p h x -> p (h x)"), ALU.mult), reads=[ptk, "Etab"], writes=[ptk])
            if first:
                pv0 = pt[:].rearrange("p (r kc q) -> p r kc q", r=4, kc=2)[:, :, 0, :]
                S.op("dve", partial(_ts1, pv0, pv0, flag[:, 0:1], ALU.mult), reads=[ptk, "flag"], writes=[ptk])
            return pt[:].rearrange("p (r kc q) -> p r kc q", r=4, kc=2), ptk

        def sw2(j, g, ptv, ptk):
            bt = slice(j * 128, (j + 1) * 128)
            po, pok = psum()
            pd, pdk = psum()
            for kc in range(2):
                if kc == 0:
                    if j == 0:
                        vv, vkey = carV[:, l, g, :], ("carV", l)
                    else:
                        vv, vkey = Vt[:, j - 1, g, :], "Vt"
                else:
                    vv, vkey = Vt[:, j, g, :], "Vt"
                S.op("pe", partial(_mm, po[:, :].rearrange("p (r q) -> p r q", r=4), vv, ptv[:, :, kc, :], kc == 0, kc == 1), reads=[vkey, ptk], writes=[pok])
            for kc in range(2):
                S.op("pe", partial(_mm, pd[:, :].rearrange("p (r q) -> p r q", r=4), ones_b[:, :], ptv[:, :, kc, :], kc == 0, kc == 1), reads=["ones_b", ptk], writes=[pdk])
            rd, rdk = scr()
            for r in range(4):
                h = 4 * g + r
                S.op("dve", partial(_ts1, rd[:, r * 128:(r + 1) * 128], pd[:, r * 128:(r + 1) * 128], es_t[:, h:h + 1], ALU.add),
                     reads=[pdk, "es"], writes=[rdk])
            S.op("act", partial(_act, rd[:, :], rd[:, :], AF.Ln, 1.0), reads=[rdk], writes=[rdk])
            S.op("act", partial(_act, rd[:, :], rd[:, :], AF.Exp, -1.0), reads=[rdk], writes=[rdk])
            for ph in range(2):
                prt = slice(ph * 64, (ph + 1) * 64)
                pov = po[prt, :].rearrange("p (c two q) -> p c two q", c=2, two=2)[:, :, ph, :]
                rdv = rd[prt, :].rearrange("p (c two q) -> p c two q", c=2, two=2)[:, :, ph, :]
                S.op("dve", partial(_tt, A1[prt, 2 * g:2 * g + 2, bt], pov, rdv, ALU.mult),
                     reads=[pok, rdk], writes=[("ob", 2 * g), ("ob", 2 * g + 1)])

        items = [(j, g) for j in range(mixlo, NBLK) for g in range(4)]
        prev = None
        for it in items:
            ctx = sw1(*it)
            if prev is not None:
                sw2(*prev)
            prev = it + ctx
        sw2(*prev)
        for e2 in range(2):
            S.op("dve", partial(_copy, carK[:, l, e2, :, :], kEO[:, e2, :, (NBLK - 1) * 128:NBLK * 128]), reads=["kTd"], writes=[("carK", l)])
        S.op("dve", partial(_copy, carV[:, l, :, :], Vt[:, NBLK - 1, :, :]), reads=["Vt"], writes=[("carV", l)])
        dbg("ob", A1[0:64, 0, 128:256], [("ob", 0)])

        wbr = [w_branch_d[l, 0].rearrange("(k p) c -> p k c", p=128), w_branch_d[l, 1].rearrange("(k p) c -> p k c", p=128),
               w_branch_d[l, 2].rearrange("(k p) c -> p k c", p=128)]
        wout = wview(w_out_d[l])
        S.switch([("q", c) for c in range(8)], [("y", c) for c in range(8)])
        for half in range(2):
            for dd in range(8):
                d = half * 8 + dd
                dc = slice(d * 128, (d + 1) * 128)

                for b in range(3):
                    def pfb(t, b=b, dc=dc):
                        return [(t[:, 0:2048].rearrange("p (k c) -> p k c", k=16), wi[:, :, 4608 + b * 2048 + dc.start:4608 + b * 2048 + dc.stop]),
                                (t[:, 2048:3072].rearrange("p (k c) -> p k c", k=8), wbr[b][:, :, dc])]
                    wt, wk = wload(pfb)
                    wg = wt[:, 0:2048].rearrange("p (k c) -> p k c", k=16)
                    wb_ = wt[:, 2048:3072].rearrange("p (k c) -> p k c", k=8)
                    src = (A0, A1, A2)[b]
                    skey = ("u", "ob", "mq")[b]
                    for s in range(2):
                        ts = TS(s)
                        pg, pgk = psum()
                        for k in range(16):
                            S.op("pe", partial(_mm, pg[:, 0:ts.stop - ts.start], wg[:, k, :], nT[:, k, ts], k == 0, k == 15), reads=[wk, ("n", k)], writes=[pgk])
                        sg, sgk = scr()
                        S.op("act", partial(_act, sg[:, 0:ts.stop - ts.start], pg[:, 0:ts.stop - ts.start], AF.Sigmoid, 1.0), reads=[pgk], writes=[sgk])
                        py, pyk = psum()
                        for k in range(8):
                            S.op("pe", partial(_mm, py[:, 0:ts.stop - ts.start], wb_[:, k, :], src[:, k, ts], k == 0, k == 7), reads=[wk, (skey, k)], writes=[pyk])
                        if b == 0:
                            S.op("dve", partial(_tt, ACC[:, s, 0:ts.stop - ts.start], sg[:, 0:ts.stop - ts.start], py[:, 0:ts.stop - ts.start], ALU.mult), reads=[sgk, pyk], writes=[("acc", s)])
                        else:
                            S.op("dve", partial(_tt, sg[:, 0:ts.stop - ts.start], sg[:, 0:ts.stop - ts.start], py[:, 0:ts.stop - ts.start], ALU.mult), reads=[sgk, pyk], writes=[sgk])
                            if b == 1:
                                S.op("dve", partial(_tt, ACC[:, s, 0:ts.stop - ts.start], ACC[:, s, 0:ts.stop - ts.start], sg[:, 0:ts.stop - ts.start], ALU.add), reads=[sgk, ("acc", s)], writes=[("acc", s)])
                            else:
                                S.op("dve", partial(_tt, A3q[:, dd, ts], ACC[:, s, 0:ts.stop - ts.start], sg[:, 0:ts.stop - ts.start], ALU.add), reads=[sgk, ("acc", s)], writes=[("y", dd)])
            for d2 in range(16):
                def pf(t, d2=d2):
                    return [(t[:, 0:1024].rearrange("p (k c) -> p k c", k=8), wout[:, half * 8:(half + 1) * 8, d2 * 128:(d2 + 1) * 128])]
                wt, wk = wload(pf)
                wv = wt[:, 0:1024].rearrange("p (k c) -> p k c", k=8)
                for s in range(2):
                    ts = TS(s)
                    po, pok = psum()
                    for k in range(8):
                        S.op("pe", partial(_mm, po[:, 0:ts.stop - ts.start], wv[:, k, :], A3q[:, k, ts], k == 0, k == 7), reads=[wk, ("y", k)], writes=[pok])
                    S.op("dve", partial(_tt, xT[:, d2, ts], po[:, 0:ts.stop - ts.start], xT[:, d2, ts], ALU.add), reads=[pok, ("x", d2)], writes=[("x", d2)])
                    if half == 1:
                        stat_push(d2, s, last=(d2 == 15 and s == 1))

    for ti in range(n_tiles):
        t0 = ti * TT
        S.op("sp", partial(_dma, [(xT[:, c, :], xT_d[c * 128:(c + 1) * 128, t0:t0 + TT]) for c in range(16)]),
             writes=XK, dma="xload", ndma=16)
        for l in range(depth):
            S.switch(MIXK, GK)
            cur["lo"] = need(ti, l, "ffn1")
            ffn(w_ffn_in[0][l], w_ffn_out[0][l], (l * 4 + 0) * 16)
            if ti == 0 and l == 0:
                dbg("h1", xT[:, 0, 128:256], [("x", 0)])
            S.switch(GK, MIXK)
            if "mixer" not in skip:
                mixer(l, ti)
            if ti == 0 and l == 0:
                dbg("h2", xT[:, 0, 128:256], [("x", 0)])
            S.switch(MIXK, GK)
            cur["lo"] = need(ti, l, "ffn2")
            if "ffn2" not in skip:
                ffn(w_ffn_in[1][l], w_ffn_out[1][l], (l * 4 + 2) * 16)
            if ti == 0 and l == 0:
                dbg("h3", xT[:, 0, 128:256], [("x", 0)])

        def fin(c, s, ts, rs, rk, t0=t0):
            o, ok = scr()
            S.op("dve", partial(_stt, o[:, 0:ts.stop - ts.start], xT[:, c, ts], gv[:, 128 + c:129 + c], rs[:, 0:ts.stop - ts.start], ALU.mult, ALU.mult), reads=[("x", c), rk, "gv"], writes=[ok])
            S.op("sp", partial(_dma, [(outT_d[c * 128:(c + 1) * 128, t0 - 256 + ts.start:t0 - 256 + ts.stop], o[:, 0:ts.stop - ts.start])]), reads=[ok], dma=("out",) + ok)
        cur["lo"] = need(ti, depth - 1, "final")
        rmsnorm(0, out_fn=fin)
    fw = [("out", "sc", i) for i in range(4)] + ["dbg_" + n for n in dbg_d]
    S.emit(final_waits=fw)
    es.close()
    return nc


def host_inputs(inp):
    f32 = np.float32
    x = np.asarray(inp["x"], f32)
    mem = np.asarray(inp["mem"], f32)
    g = lambda v: np.ascontiguousarray(np.asarray(v, f32).reshape(16, 128).T)
    gcols = []
    for l in range(2):
        for nm in ("g_ffn1", "g_mix", "g_ffn2", "g_mem"):
            gcols.append(g(inp[nm][l]))
    gcols.append(g(inp["g_final"]))
    gv = np.ascontiguousarray(np.concatenate(gcols, axis=1))
    lnp = np.zeros((2, 128, 3072), f32)
    wsT = np.zeros((2, 128, 512), f32)
    for l in range(2):
        lnp[l, :, 0:1024] = np.asarray(inp["gmlp_ln_g"][l], f32)[None, :]
        lnp[l, :, 1024:2048] = np.asarray(inp["gmlp_ln_b"][l], f32)[None, :]
        bs = np.asarray(inp["b_s"][l], f32)
        lnp[l, :, 2048:3072] = np.repeat(bs, 2, axis=0).reshape(1, 1024)
        ws = np.asarray(inp["w_s"][l], f32)
        wsT[l] = np.transpose(ws, (2, 0, 1)).reshape(128, 512)
    si, ti_ = np.arange(128)[:, None], np.arange(128)[None, :]
    m1 = (si <= ti_).astype(f32)
    maskT = np.ascontiguousarray(np.tile(m1, (1, 4)))
    sinks = np.ascontiguousarray(np.broadcast_to(np.asarray(inp["swa_sinks"], f32)[:, None, :], (2, 128, 16)))
    key, q = np.arange(128)[:, None], np.arange(128)[None, :]
    d_prev = np.where(key > q, (q + 128 - key).astype(f32), BIG).astype(f32)
    d_cur = np.where(key <= q, (q - key).astype(f32), BIG).astype(f32)
    dist_n = np.concatenate([d_prev, d_cur], axis=1)
    dist_f = np.concatenate([np.full((128, 128), BIG, f32), d_cur], axis=1)
    shared = dict(gv=gv, lnp=lnp, wsT=wsT, maskT=maskT, sinks=sinks)
    for nm in ("w_ffn1_in", "w_ffn2_in", "w_ffn1_out", "w_ffn2_out", "w_in", "w_mem_kv", "w_branch", "w_out"):
        shared[nm] = np.ascontiguousarray(np.asarray(inp[nm], f32))
    maps = []
    for core in range(8):
        b, half = core // 2, core % 2
        xw = np.zeros((WIN, D), f32)
        if half == 0:
            xw[256:] = x[b, 0:2048]
        else:
            xw[:] = x[b, 2048 - 256:4096]
        m = dict(shared)
        m["xT"] = np.ascontiguousarray(xw.T)
        m["memT"] = np.ascontiguousarray(mem[b].T)
        m["dist"] = np.ascontiguousarray(np.stack([dist_n, dist_f if half == 0 else dist_n]))
        m["flag"] = np.full((128, 1), 0.0 if half == 0 else 1.0, np.float32)
        maps.append(m)
    return maps


_NC_CACHE = {}


def kernel(**inputs):
    maps = host_inputs(inputs)
    if "nc" not in _NC_CACHE:
        _NC_CACHE["nc"] = build_nc()
    nc = _NC_CACHE["nc"]
    res = run_bass_kernel_spmd(nc, maps, core_ids=list(range(8)))
    out = np.zeros((4, 4096, D), np.float32)
    for core in range(8):
        b, half = core // 2, core % 2
        o = res.results[core]["outT"]
        out[b, half * 2048:(half + 1) * 2048, :] = o.T
    return out
```

```python
import contextlib
from functools import partial
import numpy as np
import concourse.bass as bass
import concourse.mybir as mybir
from concourse.bass_utils import run_bass_kernel_spmd

F32 = mybir.dt.float32
BF16 = mybir.dt.bfloat16
AF = mybir.ActivationFunctionType
ALU = mybir.AluOpType
AX = mybir.AxisListType
ENGS = ("pe", "act", "dve", "pool", "sp")

D = 2048
DFF = 5504
NF = 43
INW = 10752
TT = 768
NS = 384
NBLK = 6
WIN = 2304
EPS = 1e-6
BIG = 1.0e6


class Sched:
    strict_same = True
    nwaits = None

    def __init__(self, nc, same_eng_sync=True):
        self.nc = nc
        self.ops = []
        self.res = {}
        self.dma_cnt = {}
        self.same_eng_sync = same_eng_sync

    def op(self, eng, fn, reads=(), writes=(), dma=None, ndma=1):
        i = len(self.ops)
        deps = {}
        for k in reads:
            r = self.res.setdefault(k, [None, []])
            if r[0] is not None:
                deps[r[0]] = "raw"
            r[1].append(i)
        for k in writes:
            r = self.res.setdefault(k, [None, []])
            if r[0] is not None and r[0] != i:
                deps.setdefault(r[0], "waw")
            for x in r[1]:
                if x != i:
                    deps.setdefault(x, "war")
            r[0] = i
            r[1] = []
        o = dict(eng=eng, fn=fn, deps=deps, dma=dma, sig=False, idx=None, cnt=None)
        if dma is not None:
            self.dma_cnt[dma] = self.dma_cnt.get(dma, 0) + 16 * ndma
            o["cnt"] = self.dma_cnt[dma]
        self.ops.append(o)
        return i

    def switch(self, old_keys, new_keys):
        ids = set()
        for k in old_keys:
            r = self.res.pop(k, None)
            if r is None:
                continue
            if r[0] is not None:
                ids.add(r[0])
            ids.update(r[1])
        best = {}
        for i in ids:
            o = self.ops[i]
            key = ("d", o["dma"]) if o["dma"] is not None else ("e", o["eng"])
            if key not in best or best[key] < i:
                best[key] = i
        lst = list(best.values())
        for k in new_keys:
            r = self.res.setdefault(k, [None, []])
            r[1].extend(lst)

    def emit(self, final_waits=()):
        nc = self.nc
        ops = self.ops
        for o in ops:
            need = {}
            for d, kind in o["deps"].items():
                Dd = ops[d]
                if Dd["dma"] is not None:
                    need[d] = kind
                    continue
                if Dd["eng"] == o["eng"] and o["dma"] is None:
                    if o["eng"] == "pe":
                        continue
                    if not self.same_eng_sync or (kind != "raw" and not self.strict_same):
                        continue
                Dd["sig"] = True
                need[d] = kind
            o["need"] = need
        cnt = {e: 0 for e in ENGS}
        for o in ops:
            if o["sig"]:
                cnt[o["eng"]] += 1
                o["idx"] = cnt[o["eng"]]
        with contextlib.ExitStack() as es:
            esem = {e: es.enter_context(nc.semaphore("s_" + e)) for e in ENGS}
            dsem = {}
            for n, k in enumerate(self.dma_cnt):
                dsem[k] = es.enter_context(nc.semaphore("d%d" % n))
            block = es.enter_context(nc.Block())

            def run(engname):
                def body(eng):
                    seen = {}
                    for o in ops:
                        if o["eng"] != engname:
                            continue
                        for d in o["need"]:
                            Dd = ops[d]
                            if Dd["dma"] is not None:
                                s, v, sk = dsem[Dd["dma"]], Dd["cnt"], ("d", Dd["dma"])
                            else:
                                s, v, sk = esem[Dd["eng"]], Dd["idx"], ("e", Dd["eng"])
                            if seen.get(sk, 0) >= v:
                                continue
                            seen[sk] = v
                            if Sched.nwaits is not None:
                                Sched.nwaits[(engname, sk[1] if sk[0] == "e" else "dma")] = Sched.nwaits.get((engname, sk[1] if sk[0] == "e" else "dma"), 0) + 1
                            eng.wait_ge(s, v)
                        ins = o["fn"](eng)
                        if o["dma"] is not None:
                            if not isinstance(ins, (list, tuple)):
                                ins = [ins]
                            for x in ins:
                                x.then_inc(dsem[o["dma"]], 16)
                        elif o["sig"]:
                            ins.then_inc(esem[engname], 1)
                    if engname == "sp":
                        for k in [k for k in final_waits if k in dsem]:
                            eng.wait_ge(dsem[k], self.dma_cnt[k])
                return body

            block.tensor(run("pe"))
            block.scalar(run("act"))
            block.vector(run("dve"))
            block.gpsimd(run("pool"))
            block.sync(run("sp"))


def _mm(out, lhsT, rhs, st, sp, e):
    return e.matmul(out, lhsT, rhs, start=st, stop=sp)


def _dma(pairs, e):
    return [e.dma_start(out=o, in_=i) for o, i in pairs]


def _act(out, in_, func, scale, e):
    return e.activation(out, in_, func, scale=scale)


def _actb(out, in_, func, bias, scale, e):
    return e.activation(out, in_, func, bias=bias, scale=scale)


def _tt(out, a, b, op, e):
    return e.tensor_tensor(out, a, b, op)


def _ts(out, a, s1, s2, op0, op1, e):
    return e.tensor_scalar(out, a, s1, s2, op0, op1)


def _ts1(out, a, s1, op0, e):
    return e.tensor_single_scalar(out, a, s1, op0)


def _stt(out, a, s, b, op0, op1, e):
    return e.scalar_tensor_tensor(out, a, s, b, op0, op1)


def _acopy(out, a, e):
    return e.activation(out, a, AF.Identity)


def _copy(out, a, e):
    return e.tensor_copy(out, a)


def _rsum(out, a, e):
    return e.reduce_sum(out, a, AX.X)


def _recip(out, a, e):
    return e.reciprocal(out, a)


def _memset(ap, v, e):
    return e.memset(ap, v)


def build_nc(n_tiles=3, depth=2, dbg_names=(), skip=()):
    nc = bass.Bass("TRN2", target_bir_lowering=False)
    es = contextlib.ExitStack()
    es.enter_context(nc.allow_low_precision("bf16 matmul operands, fp32 accumulate"))

    def din(name, shape):
        return nc.dram_tensor(name, list(shape), F32, kind="ExternalInput").ap()

    xT_d = din("xT", [D, WIN])
    memT_d = din("memT", [D, 256])
    w_ffn_in = [din("w_ffn1_in", [2, D, 2 * DFF]), din("w_ffn2_in", [2, D, 2 * DFF])]
    w_ffn_out = [din("w_ffn1_out", [2, DFF, D]), din("w_ffn2_out", [2, DFF, D])]
    w_in_d = din("w_in", [2, D, INW])
    w_memkv_d = din("w_mem_kv", [2, D, 2048])
    w_branch_d = din("w_branch", [2, 3, 1024, D])
    w_out_d = din("w_out", [2, D, D])
    gv_d = din("gv", [128, 144])
    lnp_d = din("lnp", [2, 128, 3072])
    wsT_d = din("wsT", [2, 128, 512])
    maskT_d = din("maskT", [128, 512])
    sinks_d = din("sinks", [2, 128, 16])
    dist_d = din("dist", [2, 128, 256])
    flag_d = din("flag", [128, 1])
    outT_d = nc.dram_tensor("outT", [D, WIN - 256], F32, kind="ExternalOutput").ap()
    dbg_d = {n: nc.dram_tensor("dbg_" + n, list(shp), dt, kind="ExternalOutput").ap() for n, shp, dt in dbg_names}

    mkv_s = nc.dram_tensor("mkv_scratch", [2, 128, 4096], BF16).ap()
    S = Sched(nc)
    sb = lambda n, shp, dt: nc.alloc_sbuf_tensor("sb_" + n, shp, dt)

    xT = sb("xT", [128, 16, TT], F32)
    nT = sb("nT", [128, 16, TT], BF16)
    AR = sb("arena", [128, 32 * TT], BF16)
    A0 = AR[:, 0:8 * TT].rearrange("p (c t) -> p c t", c=8)
    A1 = AR[:, 8 * TT:16 * TT].rearrange("p (c t) -> p c t", c=8)
    A1m = AR[:, 8 * TT:8 * TT + 16 * 256].rearrange("p (c t) -> p c t", c=16)
    A2 = AR[:, 16 * TT:24 * TT].rearrange("p (c t) -> p c t", c=8)
    A3 = AR[:, 24 * TT:32 * TT]
    A3vn = A3.rearrange("p (j f) -> p j f", j=NBLK)
    A3q = A3.rearrange("p (c t) -> p c t", c=8)
    A3mk = A3[:, 0:2048].rearrange("p (c m) -> p c m", c=8)
    A3mv = A3[:, 2048:4096].rearrange("p (c f) -> p c f", c=2)
    gT = AR.rearrange("p (f t) -> p f t", f=32)
    R2 = sb("r2", [128, 4608], F32)
    lng, lnb = R2[:, 0:1024], R2[:, 1024:2048]
    bsb = R2[:, 2048:3072].rearrange("p (c t) -> p c t", c=8)
    R2b = R2[:].bitcast(BF16)
    kEO = R2b[:, 0:8 * TT].rearrange("p (e g t) -> p e g t", e=2, g=4)
    Vt = R2b[:, 8 * TT:8 * TT + NBLK * 512].rearrange("p (j g f) -> p j g f", j=NBLK, g=4)
    carK = sb("carK", [128, 2, 2, 4, 128], BF16)
    carV = sb("carV", [128, 2, 4, 128], BF16)
    wsT = sb("wsT", [128, 512], BF16)
    wsF = sb("wsF", [128, 512], F32)
    maskT = sb("maskT", [128, 512], F32)
    Etab = sb("Etab", [128, 16, 256], BF16)
    flag = sb("flag", [128, 1], F32)
    gv = sb("gv", [128, 144], F32)
    es_t = sb("es", [128, 16], F32)
    st = sb("st", [128, 8], F32)
    ones_m = sb("ones_m", [128, 128], BF16)
    epsT = sb("epsT", [128, 1], F32)
    ones_b = sb("ones_b", [128, 128], BF16)
    NSLOT = 3
    wslot = [sb("wslot%d" % i, [128, 4096], BF16) for i in range(NSLOT)]
    SC = [sb("sc%d" % i, [128, 512], F32) for i in range(4)]
    SQ = [sb("sq%d" % i, [128, NS], BF16) for i in range(4)]
    ACC = sb("acc", [128, 2, NS], F32)
    SF = [sb("sf%d" % i, [128, 1024], F32) for i in range(1)]
    SB = [sb("sbb%d" % i, [128, 1024], BF16) for i in range(2)]
    RS = [sb("rs%d" % i, [128, NS], F32) for i in range(2)]
    PS = [nc.alloc_psum_tensor("ps%d" % i, [128, 512], F32) for i in range(8)]

    ctr = dict(w=0, ps=0, sc=0, sf=0, sb=0, ms=0, rs=0, sq=0)

    cur = dict(lo=0)

    def TS(s):
        return slice(cur["lo"] * 128 if s == 0 else NS, (s + 1) * NS)

    def need(ti, l, phase):
        if ti > 0:
            return 0
        return {"ffn1": l, "kv": l, "mix": l + 1, "ffn2": l + 1, "final": depth}[phase]

    def ring(name, n):
        i = ctr[name] % n
        ctr[name] += 1
        return i

    def psum():
        i = ring("ps", 6)
        return PS[i], ("ps", i)

    STAT = [(PS[6], ("ps", 6)), (PS[7], ("ps", 7))]
    fused = dict(ready=False, pend=[])

    def stat_push(c, s, last=False):
        ts = TS(s)
        qi = ring("sq", 4)
        sq, sk = SQ[qi], ("sq", qi)
        S.op("dve", partial(_tt, sq[:, 0:ts.stop - ts.start], xT[:, c, ts], xT[:, c, ts], ALU.mult), reads=[("x", c)], writes=[sk])
        fused["pend"].append((c, s, sq, sk, ts.stop - ts.start))
        while len(fused["pend"]) > (0 if last else 3):
            c0, s0, sq0, sk0, n0 = fused["pend"].pop(0)
            S.op("pe", partial(_mm, STAT[s0][0][:, 0:n0], ones_m[:], sq0[:, 0:n0], c0 == 0, c0 == 15), reads=[sk0, "ones_m"], writes=[STAT[s0][1]])
        if last:
            fused["ready"] = True

    def scr():
        i = ring("sc", 4)
        return SC[i], ("sc", i)

    def scrF():
        i = ring("sf", 1)
        return SF[i], ("sf", i)

    def scrB():
        i = ring("sb", 2)
        return SB[i], ("sb", i)

    def wload(pairs_fn):
        i = ring("w", NSLOT)
        pairs = pairs_fn(wslot[i])
        S.op("pool", partial(_dma, pairs), writes=[("w", i)], dma=("w", i), ndma=len(pairs))
        return wslot[i], ("w", i)

    def wview(ap2d):
        return ap2d.rearrange("(k p) c -> p k c", p=128)

    S.op("sp", partial(_dma, [(gv[:], gv_d), (maskT[:], maskT_d), (flag[:], flag_d)]),
         writes=["gv", "maskT", "flag"], dma="const", ndma=3)
    dtmp, dtk = SC[0], ("sc", 0)
    S.op("sp", partial(_dma, [(dtmp[:, 0:256], dist_d[0])]), writes=[dtk], dma="const2")
    for h in range(16):
        S.op("act", partial(_act, Etab[:, h, :], dtmp[:, 0:256], AF.Exp, -(2.0 ** (-8.0 * (h + 1) / 16.0))), reads=[dtk], writes=["Etab"])
    S.op("dve", partial(_memset, ones_m[:], 1.0 / D), writes=["ones_m"])
    S.op("dve", partial(_memset, ones_b[:], 1.0), writes=["ones_b"])
    S.op("dve", partial(_memset, epsT[:], EPS), writes=["epsT"])
    S.op("dve", partial(_memset, carK[:], 0.0), writes=[("carK", 0), ("carK", 1)])
    S.op("dve", partial(_memset, carV[:], 0.0), writes=[("carV", 0), ("carV", 1)])

    def dbg(name, ap, reads):
        if name in dbg_d:
            S.op("sp", partial(_dma, [(dbg_d[name], ap)]), reads=reads, dma="dbg_" + name)

    XK = [("x", c) for c in range(16)]
    NK = [("n", c) for c in range(16)]

    def rmsnorm(gcol, out_fn=None):
        use_fused = fused["ready"]
        fused["ready"] = False

        def stats(s):
            ts = TS(s)
            if use_fused:
                pst, pk = STAT[s]
            else:
                pst, pk = psum()
                for c in range(16):
                    qi = ring("sq", 4)
                    sq, sk = SQ[qi], ("sq", qi)
                    S.op("dve", partial(_tt, sq[:, 0:ts.stop - ts.start], xT[:, c, ts], xT[:, c, ts], ALU.mult), reads=[("x", c)], writes=[sk])
                    S.op("pe", partial(_mm, pst[:, 0:ts.stop - ts.start], ones_m[:], sq[:, 0:ts.stop - ts.start], c == 0, c == 15), reads=[sk, "ones_m"], writes=[pk])
            ri = ring("rs", 2)
            rs, rk = RS[ri], ("rs", ri)
            S.op("act", partial(_actb, rs[:, 0:ts.stop - ts.start], pst[:, 0:ts.stop - ts.start], AF.Sqrt, epsT[:], 1.0), reads=[pk, "epsT"], writes=[rk])
            S.op("dve", partial(_recip, rs[:, 0:ts.stop - ts.start], rs[:, 0:ts.stop - ts.start]), reads=[rk], writes=[rk])
            return ts, rs, rk

        if out_fn is None:
            for s in range(2):
                ts, rs, rk = stats(s)
                for c in range(16):
                    S.op("dve", partial(_stt, nT[:, c, ts], xT[:, c, ts], gv[:, gcol + c:gcol + c + 1], rs[:, 0:ts.stop - ts.start], ALU.mult, ALU.mult),
                         reads=[("x", c), rk, "gv"], writes=[("n", c)])
        else:
            st2 = [stats(s) for s in range(2)]
            for c in range(16):
                for s in range(2):
                    ts, rs, rk = st2[s]
                    out_fn(c, s, ts, rs, rk)

    def ffn(w_in_l, w_out_l, gcol):
        rmsnorm(gcol)
        wi = wview(w_in_l)
        wo = w_out_l.rearrange("(f p) c -> p f c", p=128)
        for (f0, f1) in ((0, 22), (22, 43)):
            nf = f1 - f0
            def p1_load(f):
                def pf(t):
                    v = t[:, :].rearrange("p (k a c) -> p k a c", k=16, a=2)
                    return [(v[:, :, 0, :], wi[:, :, f * 128:(f + 1) * 128]),
                            (v[:, :, 1, :], wi[:, :, DFF + f * 128:DFF + (f + 1) * 128])]
                wt, wk = wload(pf)
                return wt[:, :].rearrange("p (k a c) -> p k a c", k=16, a=2), wk

            def p1_group(f, s, wv, wk):
                ts = TS(s)
                pa, pak = psum()
                pb, pbk = psum()
                for k in range(16):
                    S.op("pe", partial(_mm, pa[:, 0:ts.stop - ts.start], wv[:, k, 0, :], nT[:, k, ts], k == 0, k == 15), reads=[wk, ("n", k)], writes=[pak])
                for k in range(16):
                    S.op("pe", partial(_mm, pb[:, 0:ts.stop - ts.start], wv[:, k, 1, :], nT[:, k, ts], k == 0, k == 15), reads=[wk, ("n", k)], writes=[pbk])
                sg, sgk = scr()
                S.op("act", partial(_act, sg[:, 0:ts.stop - ts.start], pa[:, 0:ts.stop - ts.start], AF.Silu, 1.0), reads=[pak], writes=[sgk])
                S.op("dve", partial(_tt, gT[:, f - f0, ts], sg[:, 0:ts.stop - ts.start], pb[:, 0:ts.stop - ts.start], ALU.mult), reads=[sgk, pbk], writes=[("g", f - f0)])

            fl = list(range(f0, f1))
            if f0 == 0:
                head = [(f,) + p1_load(f) for f in fl[:NSLOT]]
                fl = fl[NSLOT:]
                for s in range(2):
                    for f, wv, wk in head:
                        p1_group(f, s, wv, wk)
            for f in fl:
                wv, wk = p1_load(f)
                for s in range(2):
                    p1_group(f, s, wv, wk)
            for d in range(16):
                def pf(t, d=d):
                    v = t[:, 0:nf * 128].rearrange("p (f c) -> p f c", f=nf)
                    return [(v, wo[:, f0:f1, d * 128:(d + 1) * 128])]
                wt, wk = wload(pf)
                wv = wt[:, 0:nf * 128].rearrange("p (f c) -> p f c", f=nf)
                for s in range(2):
                    ts = TS(s)
                    po, pok = psum()
                    for f in range(nf):
                        S.op("pe", partial(_mm, po[:, 0:ts.stop - ts.start], wv[:, f, :], gT[:, f, ts], f == 0, f == nf - 1), reads=[wk, ("g", f)], writes=[pok])
                    S.op("dve", partial(_stt, xT[:, d, ts], po[:, 0:ts.stop - ts.start], 0.5, xT[:, d, ts], ALU.mult, ALU.add), reads=[pok, ("x", d)], writes=[("x", d)])
                    if f0 > 0:
                        stat_push(d, s, last=(d == 15 and s == 1))

    GK = [("g", f) for f in range(22)]
    MIXK = ([("u", c) for c in range(8)] + [("ob", h) for h in range(8)] + ["memn"] + [("mq", c) for c in range(8)]
            + [("vn", j) for j in range(NBLK)] + [("q", c) for c in range(8)] + ["mk", "mv"] + [("y", c) for c in range(8)])

    def proj_chunk(wcols_fn, evac, c):
        def pf(t):
            return wcols_fn(c, t[:, 0:2048].rearrange("p (k c) -> p k c", k=16))
        wt, wk = wload(pf)
        wv = wt[:, 0:2048].rearrange("p (k c) -> p k c", k=16)
        for s in range(2):
            ts = TS(s)
            ps, pk = psum()
            for k in range(16):
                S.op("pe", partial(_mm, ps[:, 0:ts.stop - ts.start], wv[:, k, :], nT[:, k, ts], k == 0, k == 15), reads=[wk, ("n", k)], writes=[pk])
            evac(c, s, ts, ps, pk)

    def proj_fm(wcols_fn, nchunks, evac):
        for c in range(nchunks):
            proj_chunk(wcols_fn, evac, c)

    def mixer(l, tile_i):
        wi = wview(w_in_d[l])
        gcol = (l * 4 + 1) * 16
        kvlo, mixlo = need(tile_i, l, "kv"), need(tile_i, l, "mix")
        cur["lo"] = kvlo
        rmsnorm(gcol)
        cur["lo"] = mixlo
        S.op("sp", partial(_dma, [(es_t[:], sinks_d[l])]), writes=["es"], dma="tab_es")
        S.op("act", partial(_act, es_t[:], es_t[:], AF.Exp, 1.0), reads=["es"], writes=["es"])
        S.switch(["kTd", "Vt"], ["lnp"])
        S.op("sp", partial(_dma, [(R2[:, 0:3072], lnp_d[l]), (wsF[:], wsT_d[l])]), writes=["lnp", "wsF"], dma="tab_ln", ndma=2)
        S.op("dve", partial(_tt, wsT[:], wsF[:], maskT[:], ALU.mult), reads=["wsF", "maskT"], writes=["wsT"])

        for cg in range(4):
            def pf(t, cg=cg):
                return [(t[:, :].rearrange("p (k c) -> p k c", k=16), wi[:, :, 1024 + cg * 256:1024 + (cg + 1) * 256])]
            wt, wk = wload(pf)
            wv = wt[:, :].rearrange("p (k c) -> p k c", k=16)
            for j in range(mixlo, NBLK):
                ps, pk = psum()
                for k in range(16):
                    S.op("pe", partial(_mm, ps[:, 0:256], nT[:, k, j * 128:(j + 1) * 128], wv[:, k, :], k == 0, k == 15), reads=[wk, ("n", k)], writes=[pk])
                S.op("act", partial(_act, A3vn[:, j, cg * 256:(cg + 1) * 256], ps[:, 0:256], AF.Gelu, 1.0), reads=[pk], writes=[("vn", j)])

        def memnorm_emit():
            mgcol = (l * 4 + 3) * 16
            memv = memT_d.rearrange("(c p) m -> p c m", p=128)
            pst, pk = psum()
            for c in range(16):
                msb, msk = scr()
                S.op("sp", partial(_dma, [(msb[:, 0:256], memv[:, c, :])]), writes=[msk], dma=("ms",) + msk)
                qi = ring("sq", 4)
                sq, sk = SQ[qi], ("sq", qi)
                S.op("dve", partial(_tt, sq[:, 0:256], msb[:, 0:256], msb[:, 0:256], ALU.mult), reads=[msk], writes=[sk])
                S.op("pe", partial(_mm, pst[:, 0:256], ones_m[:], sq[:, 0:256], c == 0, c == 15), reads=[sk, "ones_m"], writes=[pk])
            ri = ring("rs", 2)
            rs, rk = RS[ri], ("rs", ri)
            S.op("act", partial(_actb, rs[:, 0:256], pst[:, 0:256], AF.Sqrt, epsT[:], 1.0), reads=[pk, "epsT"], writes=[rk])
            S.op("dve", partial(_recip, rs[:, 0:256], rs[:, 0:256]), reads=[rk], writes=[rk])
            for c in range(16):
                msb, msk = scr()
                S.op("sp", partial(_dma, [(msb[:, 0:256], memv[:, c, :])]), writes=[msk], dma=("ms",) + msk)
                S.op("dve", partial(_stt, A1m[:, c, :], msb[:, 0:256], gv[:, mgcol + c:mgcol + c + 1], rs[:, 0:256], ALU.mult, ALU.mult),
                     reads=[msk, rk, "gv"], writes=["memn"])


        def u_w(c, v):
            return [(v, wi[:, :, c * 128:(c + 1) * 128])]

        def u_ev(c, s, ts, ps, pk):
            S.op("act", partial(_act, A0[:, c, ts], ps[:, 0:ts.stop - ts.start], AF.Gelu, 1.0), reads=[pk], writes=[("u", c)])

        def mq_w(c, v):
            return [(v, wi[:, :, 3584 + c * 128:3584 + (c + 1) * 128])]

        def mq_ev(c, s, ts, ps, pk):
            S.op("act", partial(_acopy, A2[:, c, ts], ps[:, 0:ts.stop - ts.start]), reads=[pk], writes=[("mq", c)])
        pchunks = [partial(proj_chunk, u_w, u_ev, c) for c in range(8)] + [partial(proj_chunk, mq_w, mq_ev, c) for c in range(8)]

        def ln_blk(j):
            vj = A3vn[:, j, :]
            vk = ("vn", j)
            tf, tfk = scrF()
            S.op("dve", partial(_rsum, st[:, 0:1], vj), reads=[vk], writes=["st0"])
            S.op("dve", partial(_tt, tf[:], vj, vj, ALU.mult), reads=[vk], writes=[tfk])
            S.op("dve", partial(_rsum, st[:, 1:2], tf[:]), reads=[tfk], writes=["st1"])
            S.op("dve", partial(_ts1, st[:, 2:3], st[:, 0:1], 1.0 / 1024, ALU.mult), reads=["st0"], writes=["st2"])
            S.op("dve", partial(_tt, st[:, 3:4], st[:, 2:3], st[:, 2:3], ALU.mult), reads=["st2"], writes=["st3"])
            S.op("dve", partial(_stt, st[:, 4:5], st[:, 1:2], 1.0 / 1024, st[:, 3:4], ALU.mult, ALU.subtract), reads=["st1", "st3"], writes=["st4"])
            S.op("act", partial(_actb, st[:, 5:6], st[:, 4:5], AF.Sqrt, epsT[:], 1.0), reads=["st4", "epsT"], writes=["st5"])
            S.op("dve", partial(_recip, st[:, 5:6], st[:, 5:6]), reads=["st5"], writes=["st5"])
            S.op("dve", partial(_stt, st[:, 6:7], st[:, 2:3], -1.0, st[:, 5:6], ALU.mult, ALU.mult), reads=["st2", "st5"], writes=["st6"])
            S.op("act", partial(_actb, tf[:], vj, AF.Identity, st[:, 6:7], st[:, 5:6]), reads=[vk, "st5", "st6", tfk], writes=[tfk])
            S.op("dve", partial(_tt, tf[:], tf[:], lng, ALU.mult), reads=[tfk, "lnp"], writes=[tfk])
            S.op("dve", partial(_tt, vj, tf[:], lnb, ALU.add), reads=[tfk, "lnp"], writes=[vk])

        def mix_blk(j):
            vk = ("vn", j)
            for hc in range(2):
                ps, pk = psum()
                for cc in range(4):
                    c = hc * 4 + cc
                    g = c // 2
                    S.op("pe", partial(_mm, ps[:, cc * 128:(cc + 1) * 128], A3vn[:, j, c * 128:(c + 1) * 128], wsT[:, g * 128:(g + 1) * 128], True, True),
                         reads=[vk, "wsT"], writes=[pk])
                t2, t2k = scr()
                S.op("dve", partial(_tt, t2[:].rearrange("p (c t) -> p c t", c=4), ps[:].rearrange("p (c t) -> p c t", c=4), bsb[:, hc * 4:(hc + 1) * 4, :], ALU.add),
                     reads=[pk, "lnp"], writes=[t2k])
                uo = A0[:, hc * 4:(hc + 1) * 4, j * 128:(j + 1) * 128]
                S.op("dve", partial(_tt, uo, t2[:].rearrange("p (c t) -> p c t", c=4), uo, ALU.mult),
                     reads=[t2k] + [("u", hc * 4 + cc) for cc in range(4)], writes=[("u", hc * 4 + cc) for cc in range(4)])

        blks = list(range(mixlo, NBLK))
        nmix = 0
        for bi, j in enumerate(blks):
            ln_blk(j)
            ntake = -(-len(pchunks) // (len(blks) - bi))
            for _ in range(ntake):
                pchunks.pop(0)()
            if bi == 0 and tile_i == 0:
                memnorm_emit()
            if len(pchunks) <= 8:
                while nmix < bi:
                    mix_blk(blks[nmix])
                    nmix += 1
        while nmix < len(blks):
            mix_blk(blks[nmix])
            nmix += 1
        assert not pchunks
        dbg("oa", A0[:, 0, 128:256], [("u", 0)])

        S.switch([("vn", j) for j in range(NBLK)], ["mk", "mv"])
        S.switch(["lnp"], ["kTd", "Vt"])
        if tile_i == 0:
            wm = wview(w_memkv_d[l])
            for c in range(8):
                def pf(t, c=c):
                    return [(t[:, 0:2048].rearrange("p (k c) -> p k c", k=16), wm[:, :, c * 128:(c + 1) * 128])]
                wt, wk = wload(pf)
                wv = wt[:, 0:2048].rearrange("p (k c) -> p k c", k=16)
                ps, pk = psum()
                for k in range(16):
                    S.op("pe", partial(_mm, ps[:, 0:256], wv[:, k, :], A1m[:, k, :], k == 0, k == 15), reads=[wk, "memn"], writes=[pk])
                S.op("act", partial(_acopy, A3mk[:, c, :], ps[:, 0:256]), reads=[pk], writes=["mk"])
            for cg in range(4):
                def pf(t, cg=cg):
                    return [(t[:, :].rearrange("p (k c) -> p k c", k=16), wm[:, :, 1024 + cg * 256:1024 + (cg + 1) * 256])]
                wt, wk = wload(pf)
                wv = wt[:, :].rearrange("p (k c) -> p k c", k=16)
                for mc in range(2):
                    ps, pk = psum()
                    for k in range(16):
                        S.op("pe", partial(_mm, ps[:, 0:256], A1m[:, k, mc * 128:(mc + 1) * 128], wv[:, k, :], k == 0, k == 15), reads=[wk, "memn"], writes=[pk])
                    S.op("act", partial(_acopy, A3mv[:, mc, cg * 256:(cg + 1) * 256], ps[:, 0:256]), reads=[pk], writes=["mv"])
            S.op("sp", partial(_dma, [(mkv_s[l], A3[:, 0:4096])]), reads=["mk", "mv"], writes=[("mkvs", l)], dma=("mkvs_w", l))
        else:
            S.op("sp", partial(_dma, [(A3[:, 0:4096], mkv_s[l])]), reads=[("mkvs", l)], writes=["mk", "mv"], dma=("mkvs_r", l))

        def k_w(g, v):
            src = wi[:, :, 3072 + g * 64:3072 + (g + 1) * 64]
            return [(v[:, :, 0:64], src), (v[:, :, 64:128], src)]

        S.op("dve", partial(_memset, kEO[64:128, 0, :, :], 0.0), writes=["kTd"])
        S.op("dve", partial(_memset, kEO[0:64, 1, :, :], 0.0), writes=["kTd"])

        def k_ev(g, s, ts, ps, pk):
            S.op("act", partial(_acopy, kEO[0:64, 0, g, ts], ps[0:64, 0:ts.stop - ts.start]), reads=[pk], writes=["kTd"])
            S.op("dve", partial(_copy, kEO[64:128, 1, g, ts], ps[64:128, 0:ts.stop - ts.start]), reads=[pk], writes=["kTd"])
        cur["lo"] = kvlo
        proj_fm(k_w, 4, k_ev)

        def pf(t):
            return [(t[:, :].rearrange("p (k c) -> p k c", k=16), wi[:, :, 3328:3584])]
        wt, wk = wload(pf)
        wv = wt[:, :].rearrange("p (k c) -> p k c", k=16)
        for j in range(kvlo, NBLK):
            ps, pk = psum()
            for k in range(16):
                S.op("pe", partial(_mm, ps[:, 0:256], nT[:, k, j * 128:(j + 1) * 128], wv[:, k, :], k == 0, k == 15), reads=[wk, ("n", k)], writes=[pk])
            psv = ps[:, 0:256].rearrange("p (g d) -> p g d", g=4)
            S.op("act", partial(_acopy, Vt[:, j, :, 0:64], psv), reads=[pk], writes=["Vt"])
            S.op("dve", partial(_copy, Vt[:, j, :, 64:128], psv), reads=[pk], writes=["Vt"])

        cur["lo"] = mixlo
        def m1(s, h):
            ts = TS(s)
            pt, ptk = scrB()
            ptv = pt[:, 0:2 * NS].rearrange("p (m t) -> p m t", m=2)[:, :, 0:ts.stop - ts.start]
            for mc in range(2):
                ps, pk = psum()
                for dc in range(2):
                    S.op("pe", partial(_mm, ps[:, 0:ts.stop - ts.start], A3mk[:, 2 * h + dc, mc * 128:(mc + 1) * 128], A2[:, 2 * h + dc, ts], dc == 0, dc == 1),
                         reads=["mk", ("mq", 2 * h + dc)], writes=[pk])
                S.op("act", partial(_act, ptv[:, mc, :], ps[:, 0:ts.stop - ts.start], AF.Exp, 1.0 / 16.0), reads=[pk], writes=[ptk])
            return ptv, ptk

        def m2(s, h, ptv, ptk):
            ts = TS(s)
            pd, pdk = psum()
            for mc in range(2):
                S.op("pe", partial(_mm, pd[:, 0:ts.stop - ts.start], ones_b[:], ptv[:, mc, :], mc == 0, mc == 1), reads=[ptk, "ones_b"], writes=[pdk])
            rd, rdk = scr()
            S.op("dve", partial(_recip, rd[:, 0:ts.stop - ts.start], pd[:, 0:ts.stop - ts.start]), reads=[pdk], writes=[rdk])
            for dc in range(2):
                po, pok = psum()
                for mc in range(2):
                    S.op("pe", partial(_mm, po[:, 0:ts.stop - ts.start], A3mv[:, mc, (2 * h + dc) * 128:(2 * h + dc + 1) * 128], ptv[:, mc, :], mc == 0, mc == 1),
                         reads=["mv", ptk], writes=[pok])
                S.op("dve", partial(_tt, A2[:, 2 * h + dc, ts], po[:, 0:ts.stop - ts.start], rd[:, 0:ts.stop - ts.start], ALU.mult), reads=[pok, rdk], writes=[("mq", 2 * h + dc)])

        items = [(s, h) for s in range(2) for h in range(4)]
        prev = None
        for it in items:
            ctx = m1(*it)
            if prev is not None:
                m2(*prev)
            prev = it + ctx
        m2(*prev)
        dbg("oc", A2[:, 0, 128:256], [("mq", 0)])

        S.switch(["mk", "mv"], [("q", c) for c in range(8)])
        S.switch(["memn"], [("ob", h) for h in range(8)])

        def q_w(c, v):
            return [(v, wi[:, :, 2048 + c * 128:2048 + (c + 1) * 128])]

        def q_ev(c, s, ts, ps, pk):
            S.op("act", partial(_acopy, A3q[:, c, ts], ps[:, 0:ts.stop - ts.start]), reads=[pk], writes=[("q", c)])
        proj_fm(q_w, 8, q_ev)

        def sw1(j, g):
            first = (tile_i == 0 and j == 2)
            bt = slice(j * 128, (j + 1) * 128)
            pbank = [psum(), psum()]
            for r in range(4):
                h = 4 * g + r
                c, ph = h // 2, h % 2
                pq, pqk = pbank[r // 2]
                for kc in range(2):
                    if kc == 0:
                        if j == 0:
                            kk, kkey = carK[:, l, ph, g, :], ("carK", l)
                        else:
                            kk, kkey = kEO[:, ph, g, (j - 1) * 128:j * 128], "kTd"
                    else:
                        kk, kkey = kEO[:, ph, g, bt], "kTd"
                    o = (r % 2) * 256 + kc * 128
                    S.op("pe", partial(_mm, pq[:, o:o + 128], kk, A3q[:, c, bt], True, True), reads=[kkey, ("q", c)], writes=[pqk])
            pt, ptk = scrB()
            for bnk in range(2):
                pq, pqk = pbank[bnk]
                S.op("act", partial(_act, pt[:, bnk * 512:(bnk + 1) * 512], pq[:, :], AF.Exp, 0.125), reads=[pqk], writes=[ptk])
            S.op("dve", partial(_tt, pt[:], pt[:], Etab[:, 4 * g:4 * g + 4, :].rearrange("p h x -> p (h x)"), ALU.mult), reads=[ptk, "Etab"], writes=[ptk])
            if first:
                pv0 = pt[:].rearrange("p (r kc q) -> p r kc q", r=4, kc=2)[:, :, 0, :]
                S.op("dve", partial(_ts1, pv0, pv0, flag[:, 0:1], ALU.mult), reads=[ptk, "flag"], writes=[ptk])
            return pt[:].rearrange("p (r kc q) -> p r kc q", r=4, kc=2), ptk

        def sw2(j, g, ptv, ptk):
            bt = slice(j * 128, (j + 1) * 128)
            po, pok = psum()
            pd, pdk = psum()
            for kc in range(2):
                if kc == 0:
                    if j == 0:
                        vv, vkey = carV[:, l, g, :], ("carV", l)
                    else:
                        vv, vkey = Vt[:, j - 1, g, :], "Vt"
                else:
                    vv, vkey = Vt[:, j, g, :], "Vt"
                S.op("pe", partial(_mm, po[:, :].rearrange("p (r q) -> p r q", r=4), vv, ptv[:, :, kc, :], kc == 0, kc == 1), reads=[vkey, ptk], writes=[pok])
            for kc in range(2):
                S.op("pe", partial(_mm, pd[:, :].rearrange("p (r q) -> p r q", r=4), ones_b[:, :], ptv[:, :, kc, :], kc == 0, kc == 1), reads=["ones_b", ptk], writes=[pdk])
            rd, rdk = scr()
            for r in range(4):
                h = 4 * g + r
                S.op("act", partial(_actb, rd[:, r * 128:(r + 1) * 128], pd[:, r * 128:(r + 1) * 128], AF.Ln, es_t[:, h:h + 1], 1.0),
                     reads=[pdk, "es"], writes=[rdk])
            S.op("act", partial(_act, rd[:, :], rd[:, :], AF.Exp, -1.0), reads=[rdk], writes=[rdk])
            for ph in range(2):
                prt = slice(ph * 64, (ph + 1) * 64)
                pov = po[prt, :].rearrange("p (c two q) -> p c two q", c=2, two=2)[:, :, ph, :]
                rdv = rd[prt, :].rearrange("p (c two q) -> p c two q", c=2, two=2)[:, :, ph, :]
                S.op("dve", partial(_tt, A1[prt, 2 * g:2 * g + 2, bt], pov, rdv, ALU.mult),
                     reads=[pok, rdk], writes=[("ob", 2 * g), ("ob", 2 * g + 1)])

        items = [(j, g) for j in range(mixlo, NBLK) for g in range(4)]
        prev = None
        for it in items:
            ctx = sw1(*it)
            if prev is not None:
                sw2(*prev)
            prev = it + ctx
        sw2(*prev)
        for e2 in range(2):
            S.op("dve", partial(_copy, carK[:, l, e2, :, :], kEO[:, e2, :, (NBLK - 1) * 128:NBLK * 128]), reads=["kTd"], writes=[("carK", l)])
        S.op("dve", partial(_copy, carV[:, l, :, :], Vt[:, NBLK - 1, :, :]), reads=["Vt"], writes=[("carV", l)])
        dbg("ob", A1[0:64, 0, 128:256], [("ob", 0)])

        wbr = [w_branch_d[l, 0].rearrange("(k p) c -> p k c", p=128), w_branch_d[l, 1].rearrange("(k p) c -> p k c", p=128),
               w_branch_d[l, 2].rearrange("(k p) c -> p k c", p=128)]
        wout = wview(w_out_d[l])
        S.switch([("q", c) for c in range(8)], [("y", c) for c in range(8)])
        for half in range(2):
            for dd in range(8):
                d = half * 8 + dd
                dc = slice(d * 128, (d + 1) * 128)

                for b in range(3):
                    def pfb(t, b=b, dc=dc):
                        return [(t[:, 0:2048].rearrange("p (k c) -> p k c", k=16), wi[:, :, 4608 + b * 2048 + dc.start:4608 + b * 2048 + dc.stop]),
                                (t[:, 2048:3072].rearrange("p (k c) -> p k c", k=8), wbr[b][:, :, dc])]
                    wt, wk = wload(pfb)
                    wg = wt[:, 0:2048].rearrange("p (k c) -> p k c", k=16)
                    wb_ = wt[:, 2048:3072].rearrange("p (k c) -> p k c", k=8)
                    src = (A0, A1, A2)[b]
                    skey = ("u", "ob", "mq")[b]
                    for s in range(2):
                        ts = TS(s)
                        pg, pgk = psum()
                        for k in range(16):
                            S.op("pe", partial(_mm, pg[:, 0:ts.stop - ts.start], wg[:, k, :], nT[:, k, ts], k == 0, k == 15), reads=[wk, ("n", k)], writes=[pgk])
                        sg, sgk = scr()
                        S.op("act", partial(_act, sg[:, 0:ts.stop - ts.start], pg[:, 0:ts.stop - ts.start], AF.Sigmoid, 1.0), reads=[pgk], writes=[sgk])
                        py, pyk = psum()
                        for k in range(8):
                            S.op("pe", partial(_mm, py[:, 0:ts.stop - ts.start], wb_[:, k, :], src[:, k, ts], k == 0, k == 7), reads=[wk, (skey, k)], writes=[pyk])
                        if b == 0:
                            S.op("dve", partial(_tt, ACC[:, s, 0:ts.stop - ts.start], sg[:, 0:ts.stop - ts.start], py[:, 0:ts.stop - ts.start], ALU.mult), reads=[sgk, pyk], writes=[("acc", s)])
                        else:
                            S.op("dve", partial(_tt, sg[:, 0:ts.stop - ts.start], sg[:, 0:ts.stop - ts.start], py[:, 0:ts.stop - ts.start], ALU.mult), reads=[sgk, pyk], writes=[sgk])
                            if b == 1:
                                S.op("dve", partial(_tt, ACC[:, s, 0:ts.stop - ts.start], ACC[:, s, 0:ts.stop - ts.start], sg[:, 0:ts.stop - ts.start], ALU.add), reads=[sgk, ("acc", s)], writes=[("acc", s)])
                            else:
                                S.op("dve", partial(_tt, A3q[:, dd, ts], ACC[:, s, 0:ts.stop - ts.start], sg[:, 0:ts.stop - ts.start], ALU.add), reads=[sgk, ("acc", s)], writes=[("y", dd)])
            for d2 in range(16):
                def pf(t, d2=d2):
                    return [(t[:, 0:1024].rearrange("p (k c) -> p k c", k=8), wout[:, half * 8:(half + 1) * 8, d2 * 128:(d2 + 1) * 128])]
                wt, wk = wload(pf)
                wv = wt[:, 0:1024].rearrange("p (k c) -> p k c", k=8)
                for s in range(2):
                    ts = TS(s)
                    po, pok = psum()
                    for k in range(8):
                        S.op("pe", partial(_mm, po[:, 0:ts.stop - ts.start], wv[:, k, :], A3q[:, k, ts], k == 0, k == 7), reads=[wk, ("y", k)], writes=[pok])
                    S.op("dve", partial(_tt, xT[:, d2, ts], po[:, 0:ts.stop - ts.start], xT[:, d2, ts], ALU.add), reads=[pok, ("x", d2)], writes=[("x", d2)])
                    if half == 1:
                        stat_push(d2, s, last=(d2 == 15 and s == 1))

    for ti in range(n_tiles):
        t0 = ti * TT
        for c in range(16):
            S.op("sp", partial(_dma, [(xT[:, c, :], xT_d[c * 128:(c + 1) * 128, t0:t0 + TT])]), writes=[("x", c)], dma=("xl", c))
        for l in range(depth):
            S.switch(MIXK, GK)
            cur["lo"] = need(ti, l, "ffn1")
            ffn(w_ffn_in[0][l], w_ffn_out[0][l], (l * 4 + 0) * 16)
            if ti == 0 and l == 0:
                dbg("h1", xT[:, 0, 128:256], [("x", 0)])
            S.switch(GK, MIXK)
            if "mixer" not in skip:
                mixer(l, ti)
            if ti == 0 and l == 0:
                dbg("h2", xT[:, 0, 128:256], [("x", 0)])
            S.switch(MIXK, GK)
            cur["lo"] = need(ti, l, "ffn2")
            if "ffn2" not in skip:
                ffn(w_ffn_in[1][l], w_ffn_out[1][l], (l * 4 + 2) * 16)
            if ti == 0 and l == 0:
                dbg("h3", xT[:, 0, 128:256], [("x", 0)])

        def fin(c, s, ts, rs, rk, t0=t0):
            o, ok = scr()
            S.op("dve", partial(_stt, o[:, 0:ts.stop - ts.start], xT[:, c, ts], gv[:, 128 + c:129 + c], rs[:, 0:ts.stop - ts.start], ALU.mult, ALU.mult), reads=[("x", c), rk, "gv"], writes=[ok])
            S.op("sp", partial(_dma, [(outT_d[c * 128:(c + 1) * 128, t0 - 256 + ts.start:t0 - 256 + ts.stop], o[:, 0:ts.stop - ts.start])]), reads=[ok], dma=("out",) + ok)
        cur["lo"] = need(ti, depth - 1, "final")
        rmsnorm(0, out_fn=fin)
    fw = [("out", "sc", i) for i in range(4)] + ["dbg_" + n for n in dbg_d]
    S.emit(final_waits=fw)
    es.close()
    return nc


def host_inputs(inp):
    f32 = np.float32
    x = np.asarray(inp["x"], f32)
    mem = np.asarray(inp["mem"], f32)
    g = lambda v: np.ascontiguousarray(np.asarray(v, f32).reshape(16, 128).T)
    gcols = []
    for l in range(2):
        for nm in ("g_ffn1", "g_mix", "g_ffn2", "g_mem"):
            gcols.append(g(inp[nm][l]))
    gcols.append(g(inp["g_final"]))
    gv = np.ascontiguousarray(np.concatenate(gcols, axis=1))
    lnp = np.zeros((2, 128, 3072), f32)
    wsT = np.zeros((2, 128, 512), f32)
    for l in range(2):
        lnp[l, :, 0:1024] = np.asarray(inp["gmlp_ln_g"][l], f32)[None, :]
        lnp[l, :, 1024:2048] = np.asarray(inp["gmlp_ln_b"][l], f32)[None, :]
        bs = np.asarray(inp["b_s"][l], f32)
        lnp[l, :, 2048:3072] = np.repeat(bs, 2, axis=0).reshape(1, 1024)
        ws = np.asarray(inp["w_s"][l], f32)
        wsT[l] = np.transpose(ws, (2, 0, 1)).reshape(128, 512)
    si, ti_ = np.arange(128)[:, None], np.arange(128)[None, :]
    m1 = (si <= ti_).astype(f32)
    maskT = np.ascontiguousarray(np.tile(m1, (1, 4)))
    sinks = np.ascontiguousarray(np.broadcast_to(np.asarray(inp["swa_sinks"], f32)[:, None, :], (2, 128, 16)))
    key, q = np.arange(128)[:, None], np.arange(128)[None, :]
    d_prev = np.where(key > q, (q + 128 - key).astype(f32), BIG).astype(f32)
    d_cur = np.where(key <= q, (q - key).astype(f32), BIG).astype(f32)
    dist_n = np.concatenate([d_prev, d_cur], axis=1)
    dist_f = np.concatenate([np.full((128, 128), BIG, f32), d_cur], axis=1)
    shared = dict(gv=gv, lnp=lnp, wsT=wsT, maskT=maskT, sinks=sinks)
    for nm in ("w_ffn1_in", "w_ffn2_in", "w_ffn1_out", "w_ffn2_out", "w_in", "w_mem_kv", "w_branch", "w_out"):
        shared[nm] = np.ascontiguousarray(np.asarray(inp[nm], f32))
    maps = []
    for core in range(8):
        b, half = core // 2, core % 2
        xw = np.zeros((WIN, D), f32)
        if half == 0:
            xw[256:] = x[b, 0:2048]
        else:
            xw[:] = x[b, 2048 - 256:4096]
        m = dict(shared)
        m["xT"] = np.ascontiguousarray(xw.T)
        m["memT"] = np.ascontiguousarray(mem[b].T)
        m["dist"] = np.ascontiguousarray(np.stack([dist_n, dist_f if half == 0 else dist_n]))
        m["flag"] = np.full((128, 1), 0.0 if half == 0 else 1.0, np.float32)
        maps.append(m)
    return maps


_NC_CACHE = {}


def kernel(**inputs):
    maps = host_inputs(inputs)
    if "nc" not in _NC_CACHE:
        _NC_CACHE["nc"] = build_nc()
    nc = _NC_CACHE["nc"]
    res = run_bass_kernel_spmd(nc, maps, core_ids=list(range(8)))
    out = np.zeros((4, 4096, D), np.float32)
    for core in range(8):
        b, half = core // 2, core % 2
        o = res.results[core]["outT"]
        out[b, half * 2048:(half + 1) * 2048, :] = o.T
    return out
```
